# Optimizing a Trainium2 kernel written in Bass

```python
import math
import jax, jax.numpy as jnp
from jax import lax
import numpy as np

D_MODEL = 1024
BATCH = 4
SEQ = 4096
DEPTH = 1

MEM_LEN = 256
HEAD_DIM = 64
GMLP_HEADS = 4
ATTN_HEADS = 8
MEM_HEADS = 4
GMLP_WIDTH = GMLP_HEADS * HEAD_DIM
ATTN_WIDTH = ATTN_HEADS * HEAD_DIM
MEM_WIDTH = MEM_HEADS * HEAD_DIM
MIX_WIDTH = GMLP_WIDTH + ATTN_WIDTH + MEM_WIDTH
IN_WIDTH = 3 * GMLP_WIDTH + 4 * ATTN_WIDTH + 2 * MEM_WIDTH
CHUNK = 128
BLOCK = 128
DILATED_CONFIGS = ((128, 1), (512, 4), (2048, 16))
PAD_MULT = max(d for _, d in DILATED_CONFIGS) * BLOCK
EPS = 1e-6

kernel_name = "hybrid_gmlp_dilated_memory_layer"


def _rms(x, g):
    xf = x.astype(jnp.float32)
    y = xf * lax.rsqrt(jnp.mean(xf * xf, axis=-1, keepdims=True) + EPS)
    return (y * g.astype(jnp.float32)).astype(x.dtype)


def _dilated_branch(q, k, v, dilation, n_win):
    B, Sp, H, hd = q.shape
    L = Sp // dilation
    nb = L // BLOCK

    def to_blocks(t):
        return t.reshape(B, L, dilation, H, hd).transpose(0, 2, 3, 1, 4).reshape(B, dilation, H, nb, BLOCK, hd)

    def with_prev(t):
        prev = jnp.pad(t, ((0, 0), (0, 0), (0, 0), (1, 0), (0, 0), (0, 0)))[:, :, :, :-1]
        return jnp.concatenate([prev, t], axis=4)

    qb = to_blocks(q)
    kc = with_prev(to_blocks(k))
    vc = with_prev(to_blocks(v))
    s = jnp.einsum('bdhnqc,bdhnkc->bdhnqk', qb, kc).astype(jnp.float32) * (1.0 / math.sqrt(hd))
    qi = jnp.arange(BLOCK)[:, None] + BLOCK
    ki = jnp.arange(2 * BLOCK)[None, :]
    rel = qi - ki
    blk = jnp.arange(nb)[:, None, None]
    valid = (rel >= 0) & (rel <= n_win) & ((blk > 0) | (ki >= BLOCK))
    s = jnp.where(valid, s, -jnp.inf)
    lse = jax.nn.logsumexp(s, axis=-1)
    p = jnp.exp(s - lse[..., None])
    o = jnp.einsum('bdhnqk,bdhnkc->bdhnqc', p.astype(v.dtype), vc)
    o = o.reshape(B, dilation, H, L, hd).transpose(0, 3, 1, 2, 4).reshape(B, Sp, H, hd)
    lse = lse.reshape(B, dilation, H, L).transpose(0, 3, 1, 2).reshape(B, Sp, H)
    return o, lse


def _dilated_attention(q, k, v):
    B, S, H, hd = q.shape
    Sp = ((S + PAD_MULT - 1) // PAD_MULT) * PAD_MULT
    pad = ((0, 0), (0, Sp - S), (0, 0), (0, 0))
    qp, kp, vp = jnp.pad(q, pad), jnp.pad(k, pad), jnp.pad(v, pad)
    outs, lses = [], []
    for window, dil in DILATED_CONFIGS:
        o, l = _dilated_branch(qp, kp, vp, dil, window // dil)
        outs.append(o)
        lses.append(l)
    w = jax.nn.softmax(jnp.stack(lses, axis=0), axis=0)
    out = sum(w[i][..., None] * outs[i].astype(jnp.float32) for i in range(len(outs)))
    return out[:, :S].astype(q.dtype)


def _chunked_gmlp(u, v, v_gain, w_s, b_s):
    B, S, GH, hd = v.shape
    nc = S // CHUNK
    vn = _rms(v, v_gain).reshape(B, nc, CHUNK, GH, hd)
    tril = jnp.tril(jnp.ones((CHUNK, CHUNK), dtype=w_s.dtype))
    sp = jnp.einsum('hts,bcshd->bcthd', w_s * tril, vn) + b_s.T[:, :, None]
    return u * sp.reshape(B, S, GH, hd)


def _memory_attention(qm, mem, mem_gain, w_mem_kv, q_gain, k_gain):
    B, S, H, hd = qm.shape
    kv = _rms(mem, mem_gain) @ w_mem_kv
    mk, mv = jnp.split(kv, 2, axis=-1)
    mk = _rms(mk.reshape(B, -1, H, hd), k_gain)
    mv = mv.reshape(B, -1, H, hd)
    qn = _rms(qm, q_gain)
    s = jnp.einsum('bshc,bmhc->bhsm', qn, mk).astype(jnp.float32) * (1.0 / math.sqrt(hd))
    p = jax.nn.softmax(s, axis=-1)
    return jnp.einsum('bhsm,bmhc->bshc', p.astype(mv.dtype), mv)


def setup_inputs(seed: int = 0) -> dict:
    key = jax.random.key(seed)
    ks = jax.random.split(key, 16)
    f32 = jnp.float32
    x = jax.random.normal(ks[0], (BATCH, SEQ, D_MODEL), f32)
    mem = jax.random.normal(ks[1], (BATCH, MEM_LEN, D_MODEL), f32)
    norm_gain = 1.0 + 0.02 * jax.random.normal(ks[2], (DEPTH, D_MODEL), f32)
    w_in = jax.random.normal(ks[3], (DEPTH, D_MODEL, IN_WIDTH), f32) * D_MODEL ** -0.5
    gmlp_v_gain = 1.0 + 0.02 * jax.random.normal(ks[4], (DEPTH, GMLP_HEADS, HEAD_DIM), f32)
    gmlp_w_s = jax.random.normal(ks[5], (DEPTH, GMLP_HEADS, CHUNK, CHUNK), f32) * CHUNK ** -0.5
    gmlp_b = 1.0 + 0.02 * jax.random.normal(ks[6], (DEPTH, GMLP_HEADS, CHUNK), f32)
    attn_q_gain = 1.0 + 0.02 * jax.random.normal(ks[7], (DEPTH, HEAD_DIM), f32)
    attn_k_gain = 1.0 + 0.02 * jax.random.normal(ks[8], (DEPTH, HEAD_DIM), f32)
    mem_norm_gain = 1.0 + 0.02 * jax.random.normal(ks[9], (DEPTH, D_MODEL), f32)
    w_mem_kv = jax.random.normal(ks[10], (DEPTH, D_MODEL, 2 * MEM_WIDTH), f32) * D_MODEL ** -0.5
    mem_q_gain = 1.0 + 0.02 * jax.random.normal(ks[11], (DEPTH, HEAD_DIM), f32)
    mem_k_gain = 1.0 + 0.02 * jax.random.normal(ks[12], (DEPTH, HEAD_DIM), f32)
    w_out = jax.random.normal(ks[13], (DEPTH, MIX_WIDTH, D_MODEL), f32) * MIX_WIDTH ** -0.5
    return {"x": x, "mem": mem, "norm_gain": norm_gain, "w_in": w_in,
            "gmlp_v_gain": gmlp_v_gain, "gmlp_w_s": gmlp_w_s, "gmlp_b": gmlp_b,
            "attn_q_gain": attn_q_gain, "attn_k_gain": attn_k_gain,
            "mem_norm_gain": mem_norm_gain, "w_mem_kv": w_mem_kv,
            "mem_q_gain": mem_q_gain, "mem_k_gain": mem_k_gain, "w_out": w_out}


def reference(x, mem, norm_gain, w_in, gmlp_v_gain, gmlp_w_s, gmlp_b,
              attn_q_gain, attn_k_gain, mem_norm_gain, w_mem_kv,
              mem_q_gain, mem_k_gain, w_out):
    B, S, _ = x.shape
    split_points = np.cumsum([GMLP_WIDTH] * 3 + [ATTN_WIDTH] * 4 + [MEM_WIDTH])
    for l in range(DEPTH):
        h = _rms(x, norm_gain[l])
        proj = h @ w_in[l]
        g_u, g_v, g_gate, a_q, a_k, a_v, a_gate, m_q, m_gate = jnp.split(proj, split_points, axis=-1)

        y_g = _chunked_gmlp(g_u.reshape(B, S, GMLP_HEADS, HEAD_DIM), g_v.reshape(B, S, GMLP_HEADS, HEAD_DIM),
                            gmlp_v_gain[l], gmlp_w_s[l], gmlp_b[l]).reshape(B, S, GMLP_WIDTH)
        y_g = y_g * jax.nn.silu(g_gate)

        q = _rms(a_q.reshape(B, S, ATTN_HEADS, HEAD_DIM), attn_q_gain[l])
        k = _rms(a_k.reshape(B, S, ATTN_HEADS, HEAD_DIM), attn_k_gain[l])
        v = a_v.reshape(B, S, ATTN_HEADS, HEAD_DIM)
        y_a = _dilated_attention(q, k, v).reshape(B, S, ATTN_WIDTH) * jax.nn.silu(a_gate)

        y_m = _memory_attention(m_q.reshape(B, S, MEM_HEADS, HEAD_DIM), mem, mem_norm_gain[l], w_mem_kv[l],
                                mem_q_gain[l], mem_k_gain[l]).reshape(B, S, MEM_WIDTH)
        y_m = y_m * jax.nn.silu(m_gate)

        y = jnp.concatenate([y_g, y_a, y_m], axis=-1) @ w_out[l]
        x = x + y
    return x
```

```python
import numpy as np
import concourse.bass as bass
import concourse.mybir as mybir
from concourse.bass_utils import run_bass_kernel_spmd

F32 = mybir.dt.float32
BF16 = mybir.dt.bfloat16
AF = mybir.ActivationFunctionType
ALU = mybir.AluOpType
AX = mybir.AxisListType

NT = 2048
SL = 512
EPS = 1e-6
GU, GV, GG, AQ, AK, AV, AGT, MQ, MG = 0, 256, 512, 768, 1280, 1792, 2304, 2816, 3072


class Prog:
    ENG = ("pe", "act", "dve", "pool", "sp")

    def __init__(self, nc):
        self.nc = nc
        self.ops = []
        self.last_w = {}
        self.readers = {}
        self.ndma_sems = 16
        self.bar_from = 0
        self.default_prio = 0.0

    def op(self, eng, fn, reads=(), writes=(), dma=False, dur=None, lat=None, prio=0.0, tbl=None):
        i = len(self.ops)
        deps = set()
        for k in reads:
            w = self.last_w.get(k)
            if w is not None:
                deps.add(w)
        for k in writes:
            w = self.last_w.get(k)
            if w is not None:
                deps.add(w)
            for r in self.readers.get(k, ()):
                deps.add(r)
        deps.discard(i)
        self.ops.append(dict(eng=eng, fn=fn, deps=deps, dma=dma, dur=dur, lat=lat, fence=False,
                             prio=(prio if prio else self.default_prio), wr=list(writes), tbl=tbl))
        for k in reads:
            self.readers.setdefault(k, []).append(i)
        for k in writes:
            self.last_w[k] = i
            self.readers[k] = []
        return i

    def barrier(self):
        deps = set()
        last = {}
        for i in range(self.bar_from, len(self.ops)):
            o = self.ops[i]
            if o["dma"]:
                deps.add(i)
            else:
                last[o["eng"]] = i
        deps |= set(last.values())
        first = None
        for e in self.ENG:
            i = len(self.ops)
            d = set(deps) if first is None else {first}
            self.ops.append(dict(eng=e, fn=(lambda eng: eng.nop()), deps=d, dma=False, dur=50, lat=None, fence=True, prio=0.0, wr=["FENCE"], tbl=None))
            if first is None:
                first = i
        self.bar_from = len(self.ops)

    def schedule(self):
        ops = self.ops
        DEF = dict(pe=300, act=700, dve=700, pool=1500, sp=150)
        order = []
        seg = []
        segs = []
        for i, o in enumerate(ops):
            if o["fence"]:
                if seg:
                    segs.append(("ops", seg))
                    seg = []
                segs.append(("fence", [i]))
            else:
                seg.append(i)
        if seg:
            segs.append(("ops", seg))
        prev_seg_order = []
        for kind, seg in segs:
            if kind == "fence":
                fi = seg[0]
                last = {}
                for j in prev_seg_order:
                    if not ops[j]["dma"]:
                        last[ops[j]["eng"]] = j
                if ops[fi]["deps"] and len(ops[fi]["deps"]) > 1:
                    ops[fi]["deps"] |= set(last.values())
                order.extend(seg)
                continue
            seg_start = len(order)
            inseg = set(seg)
            ndeps = {}
            users = {}
            for i in seg:
                ds = [d for d in ops[i]["deps"] if d in inseg]
                ndeps[i] = len(ds)
                for d in ds:
                    users.setdefault(d, []).append(i)
            def dur_of(o):
                if o["dma"]:
                    return o["lat"] if o["lat"] is not None else 4000.0
                return o["dur"] if o["dur"] is not None else DEF[o["eng"]]
            blev = {}
            for i in reversed(seg):
                m = 0.0
                for u in users.get(i, ()):
                    if blev[u] > m:
                        m = blev[u]
                blev[i] = m + dur_of(ops[i]) + 250.0 + ops[i]["prio"]
            ready = [i for i in seg if ndeps[i] == 0]
            est = {}
            for i in ready:
                est[i] = 0.0
            fin = {}
            etime = {e: 0.0 for e in self.ENG}
            dma_free = [0.0]
            act_tbl = [6]
            nleft = len(seg)
            while nleft:
                best = None
                bkey = None
                for i in ready:
                    o = ops[i]
                    et = etime[o["eng"]]
                    if o["tbl"] is not None and o["tbl"] != act_tbl[0]:
                        et = et + 1300.0
                    if est[i] <= et:
                        key = (et, 0, -blev[i], i)
                    else:
                        key = (est[i], 1, -blev[i], i)
                    if bkey is None or key < bkey:
                        bkey = key
                        best = i
                i = best
                ready.remove(i)
                o = ops[i]
                st = bkey[0]
                dur = o["dur"] if o["dur"] is not None else DEF[o["eng"]]
                if o["dma"]:
                    occ = 350.0 if o["eng"] == "act" else (600.0 if o["eng"] == "pool" else 150.0)
                    etime[o["eng"]] = st + occ
                    xfer = (o["lat"] if o["lat"] is not None else 4000.0) - 2500.0
                    x0 = max(st + occ, dma_free[0])
                    dma_free[0] = x0 + xfer
                    fin[i] = dma_free[0] + 2000.0
                else:
                    etime[o["eng"]] = st + dur
                    fin[i] = st + dur
                if o["tbl"] is not None:
                    act_tbl[0] = o["tbl"]
                order.append(i)
                nleft -= 1
                for u in users.get(i, ()):
                    ndeps[u] -= 1
                    lat = 0.0 if (ops[u]["eng"] == o["eng"] and not o["dma"]) else 250.0
                    t = fin[i] + lat
                    if est.get(u, 0.0) < t:
                        est[u] = t
                    if ndeps[u] == 0:
                        ready.append(u)
            prev_seg_order = order[seg_start:]
        assert len(order) == len(ops)
        return order

    def _check_deadlock(self, ops, per_eng, dsem, sem):
        val = {}
        pos = {e: 0 for e in self.ENG}
        total = sum(len(v) for v in per_eng.values())
        done = 0
        progress = True
        while progress and done < total:
            progress = False
            for e in self.ENG:
                while pos[e] < len(per_eng[e]):
                    i = per_eng[e][pos[e]]
                    o = ops[i]
                    ok = True
                    for d in o["deps"]:
                        po = ops[d]
                        if po["sig"] is None:
                            continue
                        if po["eng"] == "pe" and e == "pe" and not po["dma"]:
                            continue
                        s, v = po["sig"]
                        if val.get(id(s), 0) < v:
                            ok = False
                            break
                    if ok and o["dma"]:
                        j = o["dma_idx"]
                        if j >= self.ndma_sems:
                            s = dsem[e][j % self.ndma_sems]
                            if val.get(id(s), 0) < 16 * (j // self.ndma_sems):
                                ok = False
                    if not ok:
                        break
                    if o["sig"] is not None:
                        s, v = o["sig"]
                        val[id(s)] = val.get(id(s), 0) + (16 if o["dma"] else 1)
                        assert val[id(s)] == v or o["dma"], (i, val[id(s)], v)
                    pos[e] += 1
                    done += 1
                    progress = True
        if done < total:
            stuck = {e: (per_eng[e][pos[e]] if pos[e] < len(per_eng[e]) else None) for e in self.ENG}
            raise RuntimeError("semaphore deadlock; stuck ops: %r" % (stuck,))

    def emit(self, reorder=True):
        nc = self.nc
        ops = self.ops
        order = self.schedule() if reorder else list(range(len(ops)))
        needed = set()
        for i, o in enumerate(ops):
            for d in o["deps"]:
                po = ops[d]
                if po["eng"] == "pe" and o["eng"] == "pe" and not po["dma"]:
                    continue
                needed.add(d)
        sem = {e: nc.alloc_semaphore("s_" + e) for e in ("pe", "act", "dve", "pool", "sp")}
        dsem = {}
        for e in ("sp", "pool", "act"):
            dsem[e] = [nc.alloc_semaphore("d_%s%d" % (e, j)) for j in range(self.ndma_sems)]
        cnt = {e: 0 for e in sem}
        dcnt = {e: 0 for e in dsem}
        for i in order:
            o = ops[i]
            if o["dma"]:
                e = o["eng"]
                j = dcnt[e]
                dcnt[e] += 1
                o["sig"] = (dsem[e][j % self.ndma_sems], 16 * (j // self.ndma_sems + 1))
                o["dma_idx"] = j
            elif i in needed:
                cnt[o["eng"]] += 1
                o["sig"] = (sem[o["eng"]], cnt[o["eng"]])
            else:
                o["sig"] = None
        per_eng = {e: [] for e in self.ENG}
        for i in order:
            per_eng[ops[i]["eng"]].append(i)

        self._check_deadlock(ops, per_eng, dsem, sem)
        import os as _os
        if _os.environ.get("KDEBUG"):
            for e in self.ENG:
                print("ENGINE", e, len(per_eng[e]))
                for i in per_eng[e][:int(_os.environ.get("KDEBUG"))]:
                    print("   ", i, ops[i]["wr"][:3], "dma" if ops[i]["dma"] else "")

        def run_engine(ename, eng):
            waited = {}
            for i in per_eng[ename]:
                o = ops[i]
                waits = {}
                for d in o["deps"]:
                    po = ops[d]
                    if po["sig"] is None:
                        continue
                    if po["eng"] == "pe" and ename == "pe" and not po["dma"]:
                        continue
                    s, v = po["sig"]
                    key = id(s)
                    if waits.get(key, (None, -1))[1] < v:
                        waits[key] = (s, v)
                if o["dma"]:
                    j = o["dma_idx"]
                    if j >= self.ndma_sems:
                        s = dsem[ename][j % self.ndma_sems]
                        v = 16 * (j // self.ndma_sems)
                        key = id(s)
                        if waits.get(key, (None, -1))[1] < v:
                            waits[key] = (s, v)
                for key, (s, v) in waits.items():
                    if waited.get(key, -1) >= v:
                        continue
                    eng.wait_ge(s, v)
                    waited[key] = v
                ins = o["fn"](eng)
                if o["sig"] is not None:
                    s, v = o["sig"]
                    ins.then_inc(s, 16 if o["dma"] else 1)

        with nc.Block() as block:
            @block.tensor
            def _(e):
                run_engine("pe", e)

            @block.scalar
            def _(e):
                run_engine("act", e)

            @block.vector
            def _(e):
                run_engine("dve", e)

            @block.gpsimd
            def _(e):
                run_engine("pool", e)

            @block.sync
            def _(e):
                run_engine("sp", e)


def build_nc():
    nc = bass.Bass("TRN2", target_bir_lowering=False)
    P = Prog(nc)

    def din(name, shape, dt=F32):
        return nc.dram_tensor(name, list(shape), dt, kind="ExternalInput")

    xo = din("xo", [NT, 1024])
    xh = din("xh", [NT, 1024])
    hm_d = din("hm", [128, 1])
    mem_d = din("mem", [256, 1024])
    w_in = din("w_in", [1024, 3328])
    w_out = din("w_out", [1024, 1024])
    w_kv = din("w_kv", [1024, 512])
    ng_d = din("ng", [128, 8])
    mng_d = din("mng", [128, 8])
    vg_d = din("vg", [128, 2])
    wsT_d = din("wsT", [128, 512])
    bT_d = din("bT", [128, 256])
    ag_d = din("ag", [128, 2])
    mg_d = din("mg", [128, 2])
    cid_d = din("cid", [128, 256])
    cmask_d = din("cmask", [128, 1024])
    ctril_d = din("ctril", [128, 512])
    out_d = nc.dram_tensor("out", [NT, 1024], F32, kind="ExternalOutput")
    scr = nc.dram_tensor("scr", [4096, 768], BF16, kind="Internal")
    hscr = nc.dram_tensor("hscr", [4 * 128, 4096], BF16, kind="Internal")

    def sb(name, cols, dt):
        return nc.alloc_sbuf_tensor(name, [128, cols], dt)

    def AP(t, p0, npart, col, dims):
        Fr = t.shape[1]
        return bass.AP(t, p0 * Fr + col, [[Fr, npart]] + [list(d) for d in dims])

    def fsz(ap):
        n = 1
        for s_ in ap.shape[1:]:
            n *= s_
        return n

    def ACT(out, in_, func, reads, writes, **kw):
        tbl = 0 if func == AF.Tanh else (6 if func == AF.Ln else None)
        P.op("act", lambda e: e.activation(out=out, in_=in_, func=func, **kw), reads, writes, dur=230 + 0.84 * fsz(in_), tbl=tbl)

    def TS(eng, out, in0, s1, s2, op0, op1, reads, writes):
        if op1 is None:
            P.op(eng, lambda e: e.tensor_scalar(out=out, in0=in0, scalar1=s1, scalar2=None, op0=op0), reads, writes, dur=120 + 0.6 * fsz(in0))
        else:
            P.op(eng, lambda e: e.tensor_scalar(out=out, in0=in0, scalar1=s1, scalar2=s2, op0=op0, op1=op1), reads, writes, dur=120 + 0.6 * fsz(in0))

    def TT(eng, out, in0, in1, op, reads, writes):
        P.op(eng, lambda e: e.tensor_tensor(out=out, in0=in0, in1=in1, op=op), reads, writes, dur=120 + 1.05 * fsz(in0))

    def STT(out, in0, scalar, in1, op0, op1, reads, writes):
        P.op("dve", lambda e: e.scalar_tensor_tensor(out=out, in0=in0, scalar=scalar, in1=in1, op0=op0, op1=op1), reads, writes, dur=120 + 1.05 * fsz(in0))

    def COPY(eng, out, in_, reads, writes):
        P.op(eng, lambda e: e.tensor_copy(out=out, in_=in_), reads, writes, dur=120 + 0.6 * fsz(in_))

    def MSET(eng, ap, val, reads, writes):
        P.op(eng, lambda e: e.memset(ap, val), reads, writes, dur=300 + 0.9 * fsz(ap))

    def MM(lst, reads, writes):
        lst = list(lst)

        def f(e):
            ins = None
            for (o, l, r, st, sp, sk) in lst:
                if sk:
                    ins = e.matmul(o, lhsT=l, rhs=r, start=st, stop=sp, skip_group_check=True)
                else:
                    ins = e.matmul(o, lhsT=l, rhs=r, start=st, stop=sp)
            return ins
        d_ = 0.0
        for (o, l, r, st, sp, sk) in lst:
            n_ = fsz(r)
            d_ += max(55 + 0.43 * n_, 150.0) if n_ > 32 else 75.0
        P.op("pe", f, reads, writes, dur=d_)

    def TR(lst, reads, writes):
        lst = list(lst)

        def f(e):
            ins = None
            for (o, i_) in lst:
                ins = e.transpose(out=o, in_=i_, identity=ident[:, :])
            return ins
        P.op("pe", f, list(reads) + ["ident"], writes, dur=130.0 * len(lst))

    def DMA(q, out, in_, reads, writes, prio=0.0):
        nbytes = fsz(out) * out.shape[0] * (2 if out.dtype == BF16 else 4)
        P.op(q, lambda e: e.dma_start(out=out, in_=in_), reads, writes, dma=True, lat=2500.0 + nbytes / 150.0, prio=prio)

    def ASEL(out, pattern, cm, reads, writes):
        P.op("pool", lambda e: e.affine_select(out=out, in_=out, pattern=pattern, compare_op=ALU.is_ge, fill=0.0,
                                               base=0, channel_multiplier=cm), reads, writes)

    KT = sb("KT", 4 * 4096, BF16)
    QT = sb("QT", 4 * NT, BF16)
    Wout = sb("Wout", 8 * 1024, BF16)
    W1b = sb("W1b", 8 * 1792, BF16)
    xbuf = [sb("xbuf%d" % i, 1024, F32) for i in range(2)]
    ident = sb("ident", 128, BF16)
    blk = sb("blk", 128, BF16)
    cst = sb("cst", 64, F32)
    stats = sb("stats", 3 * 80, F32)
    gst = sb("gst", 96, F32)
    C_HM, C_NG, C_MNG, C_VG, C_AG, C_MG, C_GQ, C_GMK = 0, 1, 9, 17, 19, 21, 23, 24

    def cs(c, n=1):
        return cst[:, c:c + n]

    RB = 104 * 1024
    R = nc.alloc_sbuf_tensor("R", [128, RB // 2], BF16)
    Rf = R.bitcast(F32)

    class Carve:
        def __init__(self):
            self.off = 0

        def take(self, nbytes):
            o = self.off
            self.off += (nbytes + 63) // 64 * 64
            assert self.off <= RB, self.off
            return o

    def rb16(off_bytes, col=0, npart=128, p0=0, dims=None, n=None):
        base = off_bytes // 2 + col
        if dims is None:
            dims = [[1, n]]
        return bass.AP(R, p0 * (RB // 2) + base, [[RB // 2, npart]] + [list(d) for d in dims])

    def rf32(off_bytes, col=0, npart=128, p0=0, dims=None, n=None):
        base = off_bytes // 4 + col
        if dims is None:
            dims = [[1, n]]
        return bass.AP(Rf, p0 * (RB // 4) + base, [[RB // 4, npart]] + [list(d) for d in dims])

    psA = nc.alloc_psum_tensor("psA", [128, 2048], F32)
    psB = nc.alloc_psum_tensor("psB", [128, 2048], F32)
    psBb = psB.bitcast(BF16)

    def bank_ap(i, col=0, n=512, p0=0, npart=128, dims=None):
        t = psA if i < 4 else psB
        c = (i % 4) * 512 + col
        if dims is None:
            dims = [[1, n]]
        return bass.AP(t, p0 * 2048 + c, [[2048, npart]] + [list(d) for d in dims])

    bank_rr = [0]
    nbanks = [8]
    held = set()

    def next_bank():
        while True:
            b = bank_rr[0]
            bank_rr[0] = (b + 1) % nbanks[0]
            if b not in held:
                return b

    psAb = psA.bitcast(BF16)

    def pT_ap(bk, col, n):
        tb = psAb if bk < 4 else psBb
        return bass.AP(tb, (bk % 4) * 1024 + col, [[4096, 128], [1, n]])

    def pT_all(bk):
        tb = psAb if bk < 4 else psBb
        return bass.AP(tb, (bk % 4) * 1024, [[4096, 128], [128, 8], [1, 128]])

    DMA("sp", cs(C_HM), hm_d[:, :], [], ["c_hm"])
    DMA("sp", cs(C_NG, 8), ng_d[:, :], [], ["c_ng"])
    DMA("sp", cs(C_MNG, 8), mng_d[:, :], [], ["c_mng"])
    DMA("sp", cs(C_VG, 2), vg_d[:, :], [], ["c_vg"])
    DMA("sp", cs(C_AG, 2), ag_d[:, :], [], ["c_ag"])
    DMA("sp", cs(C_MG, 2), mg_d[:, :], [], ["c_mg"])
    TS("dve", cs(C_GQ), cs(C_AG), cs(C_AG + 1), None, ALU.mult, None, ["c_ag"], ["c_gq"])
    TS("dve", cs(C_GQ), cs(C_GQ), 0.125, None, ALU.mult, None, ["c_gq"], ["c_gq"])
    TS("dve", cs(C_GMK), cs(C_MG), cs(C_MG + 1), None, ALU.mult, None, ["c_mg"], ["c_gmk"])
    TS("dve", cs(C_GMK), cs(C_GMK), 0.125, None, ALU.mult, None, ["c_gmk"], ["c_gmk"])
    idf = xbuf[1]
    DMA("sp", idf[:, 0:256], cid_d[:, :], [], [("x", 1)])
    ACT(ident[:, :], idf[:, 0:128], AF.Copy, [("x", 1)], ["ident"])
    ACT(blk[:, :], idf[:, 128:256], AF.Copy, [("x", 1)], ["blk"])
    KTf = KT.bitcast(F32)
    P.op("act", lambda e: e.memzero(KTf[:, :]), [], ["KTz"], dur=7000.0)
    wsTb = sb("wsTb", 512, BF16)

    wq = [0]
    tile_ctr = [0]

    xpool = [(lambda n, t_=xbuf[0]: t_[:, 0:n], ("x", 0)), (lambda n, t_=xbuf[1]: t_[:, 0:n], ("x", 1))]
    xstate = dict(pool=list(xpool))

    def xs_alloc():
        pl = xstate["pool"]
        i = wq[0] % len(pl)
        wq[0] += 1
        return pl[i]

    def load_weight_piece(q, ceng, dram, row0, col0, ncols, dst_ap, scal, key, rd=(), prio=0.0):
        xf, xk = xs_alloc()
        DMA(q, xf(ncols), dram[row0:row0 + 128, col0:col0 + ncols], [], [xk], prio=prio)
        TS(ceng, dst_ap, xf(ncols), scal, None, ALU.mult, None, [xk] + list(rd), [key])

    class Ctx:
        pass

    def x_tile(cx, src, row0, hT_out_ap, ktag, prio=0.0, extra_r=(), tbank=None):
        t = tile_ctr[0]
        tile_ctr[0] += 1
        xf, xk = xs_alloc()
        xfull = xf(1024)
        sc = (t % 80) * 3
        hbi = t % 2
        hb_ap = rb16(cx.hb[hbi], 0, n=1024)
        DMA("sp", xfull, src[row0:row0 + 128, :], list(extra_r), [xk], prio=prio)
        ACT(rb16(cx.junk, 0, n=1024), xfull, AF.Square, [xk], [("st", t, 0), "junk"], accum_out=stats[:, sc:sc + 1])
        ACT(stats[:, sc + 1:sc + 2], stats[:, sc:sc + 1], AF.Ln, [("st", t, 0)], [("st", t, 1)], scale=1.0 / 1024, bias=EPS)
        ACT(stats[:, sc + 2:sc + 3], stats[:, sc + 1:sc + 2], AF.Exp, [("st", t, 1)], [("st", t, 2)], scale=-0.5)
        TS("dve", hb_ap, xfull, stats[:, sc + 2:sc + 3], None, ALU.mult, None, [xk, ("st", t, 2)], [("hb", hbi)])
        bk = next_bank() if tbank is None else tbank
        TR([(pT_ap(bk, k * 128, 128), rb16(cx.hb[hbi], k * 128, n=128)) for k in range(8)], [("hb", hbi)], [("bank", bk)])
        COPY("dve", hT_out_ap, pT_all(bk), [("bank", bk)], [ktag])

    def proj_f2(wap_fn, wkeys, hTo, ncols, hkeys, hstride=512):
        b = next_bank()
        MM([(bank_ap(b, 0, ncols), wap_fn(kc), rb16(hTo, kc * hstride, n=ncols), kc == 0, kc == 7, False) for kc in range(8)],
           list(wkeys) + list(hkeys), [("bank", b)])
        return b

    def unit_norm(cx, b, ncols, out_ap, gain_ap, gkeys, okeys):
        sq = cx.bpool()
        sq_ap = rb16(cx.b[sq], 0, n=ncols)
        ACT(sq_ap, bank_ap(b, 0, ncols), AF.Square, [("bank", b)], [("bp", sq)])
        b2 = next_bank()
        MM([(bank_ap(b2, 0, ncols), blk[:, :], sq_ap, True, True, False)], [("bp", sq), "blk"], [("bank", b2)])
        f1 = cx.fpool()
        f1_ap = rf32(cx.f[f1], 0, n=ncols)
        ACT(f1_ap, bank_ap(b2, 0, ncols), AF.Ln, [("bank", b2)], [("fp", f1)], bias=EPS)
        f2 = cx.fpool()
        f2_ap = rf32(cx.f[f2], 0, n=ncols)
        ACT(f2_ap, f1_ap, AF.Exp, [("fp", f1)], [("fp", f2)], scale=-0.5)
        sc = gain_ap if gain_ap is not None else 1.0
        STT(out_ap, bank_ap(b, 0, ncols), sc, f2_ap, ALU.mult, ALU.mult, [("bank", b), ("fp", f2)] + list(gkeys), list(okeys))

    def make_ctx(cv, nf, nb):
        cx = Ctx()
        cx.hT = [cv.take(8 * 512 * 2) for _ in range(2)]
        cx.f = [cv.take(2048) for _ in range(nf)]
        cx.b = [cv.take(1024) for _ in range(nb)]
        cx.hb = [cv.take(2048) for _ in range(2)]
        cx.junk = cv.take(2048)
        fr = [0]
        br = [0]

        def fpool():
            i = fr[0] % len(cx.f)
            fr[0] = i + 1
            return i

        def bpool():
            i = br[0] % len(cx.b)
            br[0] = i + 1
            return i
        cx.fpool = fpool
        cx.bpool = bpool
        return cx

    cv = Carve()
    o_W1a = cv.take(8 * 1536 * 2)
    c1 = make_ctx(cv, 4, 3)
    o_vst = [cv.take(768 * 2) for _ in range(2)]
    o_xs = [cv.take(4096) for _ in range(4)]
    xstate["pool"] = list(xpool) + [((lambda n, o_=o_: rf32(o_, 0, n=n)), ("xs", k_)) for k_, o_ in enumerate(o_xs)]
    p1a_end = cv.off
    cv2 = Carve()
    VBN = 69 * 192
    o_vb1 = cv2.take(VBN * 2)
    o_pt = [cv2.take(2048) for _ in range(3)]
    o_oc = cv2.take(8192)
    o_rz = cv2.take(8192)
    o_kc4 = cv2.take(20 * 128 * 2)
    o_qc4 = cv2.take(16 * 128 * 2)
    o_kc16 = cv2.take(32 * 128 * 2)
    o_qc16 = cv2.take(16 * 128 * 2)
    assert cv2.off <= p1a_end
    p2_low_end = cv2.off
    cv2.off = p1a_end
    o_vb0 = cv2.take(VBN * 2)
    o_mask = cv2.take(2048)
    c1.f.extend([o_vb0 + k_ * 2048 for k_ in range(4)])
    c1.b.extend([o_vb0 + 8192 + k_ * 1024 for k_ in range(3)])
    o_vb = [o_vb0, o_vb1]
    o_mf = o_xs[3]
    mask_ap = rb16(o_mask, 0, n=1024)

    def late_consts():
        fa, ka = xs_alloc()
        fb, kb = xs_alloc()
        DMA("sp", fa(512), wsT_d[:, :], [], [ka])
        DMA("sp", fb(512), ctril_d[:, :], [], [kb])
        TT("dve", fa(512), fa(512), fb(512), ALU.mult, [ka, kb], [ka])
        ACT(wsTb[:, :], fa(512), AF.Copy, [ka], ["wsT"])
        fc, kc_ = xs_alloc()
        DMA("sp", fc(1024), cmask_d[:, :], [], [kc_])
        ACT(mask_ap, fc(1024), AF.Copy, [kc_], ["mask"])

    ngc = lambda kc: cs(C_NG + kc)
    def load_w1a_k():
        for kc in range(8):
            load_weight_piece("sp", "dve", w_in, kc * 128, AK, 512, rb16(o_W1a, kc * 1536 + 512, n=512), ngc(kc),
                              ("W1a", kc, 0), rd=["c_ng"])

    def load_w1a_v():
        for kc in range(8):
            load_weight_piece("sp", "dve", w_in, kc * 128, AV, 512, rb16(o_W1a, kc * 1536 + 1024, n=512), ngc(kc),
                              ("W1a", kc, 2), rd=["c_ng"])

    W1A_ALL = [("W1a", kc, j) for kc in range(8) for j in range(2)]
    W1A_KV = [("W1a", kc, 0) for kc in range(8)]
    W1A_V = [("W1a", kc, 2) for kc in range(8)]

    def w1a_ap(c0):
        return lambda kc: rb16(o_W1a, kc * 1536 + c0, n=128)

    for i in range(2):
        MSET("dve", rb16(o_vst[i], 0, n=768), 1.0, [], [("vst", i)])
        onec = rb16(o_vst[i], 64, dims=[[192, 4], [1, 64]])
        TS("dve", onec, onec, cs(C_HM), None, ALU.mult, None, [("vst", i), "c_hm"], [("vst", i)])
    vst_ctr = [0]

    def v_tile(hTo, tcol, hkey, scr_row0, halo):
        b = next_bank()
        MM([(bank_ap(b, 0, 512), rb16(hTo, kc * 512 + tcol, n=128), rb16(o_W1a, kc * 1536 + 1024, n=512), kc == 0, kc == 7, False)
            for kc in range(8)], W1A_V + [hkey], [("bank", b)])
        vi = vst_ctr[0] % 2
        vst_ctr[0] += 1
        outap = rb16(o_vst[vi], 0, dims=[[192, 4], [128, 2], [1, 64]])
        inap = bank_ap(b, 0, dims=[[128, 4], [64, 2], [1, 64]])
        if halo:
            ACT(outap, inap, AF.Copy, [("bank", b), "c_hm"], [("vst", vi)], scale=cs(C_HM))
        else:
            ACT(outap, inap, AF.Copy, [("bank", b)], [("vst", vi)])
        DMA("act", scr[scr_row0:scr_row0 + 128, :], rb16(o_vst[vi], 0, n=768), [("vst", vi)], [("scr", scr_row0 // 128)])

    def slab_1a_x(si, tiles=(0, 1, 2, 3)):
        halo = si < 4
        src = xh if halo else xo
        row_base = (si % 4) * SL
        hb_i = si % 2
        hTo = c1.hT[hb_i]
        for t in tiles:
            x_tile(c1, src, row_base + t * 128, rb16(hTo, t * 128, dims=[[512, 8], [1, 128]]), ("hT", hb_i, t),
                   prio=(1e6 if si == 0 else 0.0))

    def slab_1a(si):
        halo = si < 4
        hb_i = si % 2
        hTo = c1.hT[hb_i]
        hkeys = [("hT", hb_i, t) for t in range(4)]
        if si > 0:
            slab_1a_x(si)
        if not halo:
            so_ = si - 4
            DMA("sp", hscr[so_ * 128:(so_ + 1) * 128, :], rb16(hTo, 0, n=4096), hkeys, [("hscr", so_)])
        jobs = [("K", c) for c in range(4)]
        if not halo:
            jobs += [("Q", c) for c in range(4)]

        def do_proj(job):
            kind, c = job
            if kind == "K":
                return proj_f2(w1a_ap(512 + c * 128), W1A_KV, hTo, 512, hkeys)
            return proj_f2(w1a_ap(c * 128), W1A_ALL, hTo, 512, hkeys)

        def do_norm(job, b):
            kind, c = job
            if kind == "K":
                unit_norm(c1, b, 512, AP(KT, 0, 128, c * 4096 + si * SL, [[1, 512]]), None, ["KTz"], [("KT", c, si)])
            else:
                so = si - 4
                unit_norm(c1, b, 512, AP(QT, 0, 128, c * NT + so * SL, [[1, 512]]), cs(C_GQ), ["c_gq"],
                          [("QT", 2 * c, so), ("QT", 2 * c + 1, so)])
        bcur = do_proj(jobs[0])
        vt = 0
        for u in range(len(jobs)):
            held.add(bcur)
            bnext = do_proj(jobs[u + 1]) if u + 1 < len(jobs) else None
            if bnext is not None:
                held.add(bnext)
            if vt < 4 and (u % max(1, len(jobs) // 4) == 0):
                v_tile(hTo, vt * 128, ("hT", hb_i, vt), si * SL + vt * 128, halo)
                vt += 1
            do_norm(jobs[u], bcur)
            held.discard(bcur)
            bcur = bnext
        while vt < 4:
            v_tile(hTo, vt * 128, ("hT", hb_i, vt), si * SL + vt * 128, halo)
            vt += 1
        held.clear()

    wpieces = []
    for kc in range(8):
        wpieces.append((w_in, kc * 128, 0, 768, AP(W1b, 0, 128, kc * 1792, [[1, 768]]), ngc(kc), ("W1b", kc, 0), ["c_ng"]))
        wpieces.append((w_in, kc * 128, AGT, 1024, AP(W1b, 0, 128, kc * 1792 + 768, [[1, 1024]]), ngc(kc), ("W1b", kc, 1), ["c_ng"]))
    for kc in range(8):
        wpieces.append((w_out, kc * 128, 0, 1024, AP(Wout, 0, 128, kc * 1024, [[1, 1024]]), 0.5, ("Wout", kc), []))
    slab_1a_x(0)
    load_w1a_k()
    load_w1a_v()
    w1b_next = [0]

    def pump_w1b(n):
        for _ in range(n):
            j = w1b_next[0]
            if j >= 0:
                return
            w1b_next[0] += 1
            dram, row0, col0, ncols, dst, scal, key, rd = wpieces[j]
            xf, xk = xs_alloc()
            DMA("sp", xf(ncols), dram[row0:row0 + 128, col0:col0 + ncols], [], [xk])
            P.op("dve", lambda e, dst=dst, src_=xf(ncols), scal=scal: e.tensor_scalar(out=dst, in0=src_, scalar1=scal, scalar2=None, op0=ALU.mult),
                 [xk] + list(rd), [key], dur=120 + 0.6 * ncols, prio=2e5)

    for si in range(4):
        slab_1a(si)
        pump_w1b(2)
        if si == 3:
            late_consts()
        if si == 2:
            for kc in range(8):
                load_weight_piece("sp", "dve", w_in, kc * 128, AQ, 512, rb16(o_W1a, kc * 1536, n=512), ngc(kc),
                                  ("W1a", kc, 1), rd=["c_ng"])
    for i in range(2):
        MSET("dve", rb16(o_vst[i], 64, dims=[[192, 4], [1, 64]]), 1.0, [("vst", i)], [("vst", i)])
    for si in range(4, 8):
        slab_1a(si)
        pump_w1b(2)

    wstate = dict(next_dma=0, next_cv=0)

    SCR_ALL = [("scr", i) for i in range(32)]

    def load_vblocks(p, vbi):
        vo = o_vb[vbi]
        key = ("vb", vbi)

        def sap(off_rows, dims):
            return bass.AP(scr, off_rows * 768 + p * 192, [list(d) for d in dims])
        for (n0, cnt_) in ((15, 1), (16, 4), (20, 4), (24, 4), (28, 4)):
            DMA("sp", rb16(vo, (n0 - 15) * 192, dims=[[192, cnt_], [1, 192]]),
                sap(n0 * 128, [[768, 128], [128 * 768, cnt_], [1, 192]]), [("scr", n_) for n_ in range(n0, n0 + cnt_)],
                [("vb", vbi, "a", n0)])
        for n in range(3, 8):
            DMA("sp", rb16(vo, (17 + (n - 3) * 4) * 192, dims=[[192, 4], [1, 192]]),
                sap(4 * 128 * n, [[4 * 768, 128], [768, 4], [1, 192]]), [("scr", n_) for n_ in range(4 * n, 4 * n + 4)],
                [("vb", vbi, "b", n)])
        for n in range(2):
            for r0 in (0, 8):
                DMA("sp", rb16(vo, (37 + n * 16 + r0) * 192, dims=[[192, 8], [1, 192]]),
                    sap(16 * 128 * n + r0, [[16 * 768, 128], [768, 8], [1, 192]]), [("scr", n_) for n_ in range(16 * n, 16 * n + 16)],
                    [("vb", vbi, "c", n, r0)])

    def vkey(vbi, d, r, n):
        if d == 1:
            return ("vb", vbi, "a", 15 if n == 15 else 16 + 4 * ((n - 16) // 4))
        if d == 4:
            return ("vb", vbi, "b", n)
        return ("vb", vbi, "c", n, 8 * (r // 8))

    P.barrier()
    xstate["pool"] = list(xpool)
    stg_aps = [xbuf[0], xbuf[1]]

    def stg(i, ncols):
        i = i % 2
        return xbuf[i][:, 0:ncols], ("x", i)

    def pump_weights(flush=False):
        while True:
            did = False
            if wstate["next_cv"] < wstate["next_dma"] and (flush or wstate["next_dma"] - wstate["next_cv"] >= 2 or wstate["next_dma"] == len(wpieces)):
                j = wstate["next_cv"]
                dram, row0, col0, ncols, dst, scal, key, rd = wpieces[j]
                sap_, skey = stg(j, ncols)
                TS("dve", dst, sap_, scal, None, ALU.mult, None, [skey] + list(rd), [key])
                wstate["next_cv"] += 1
                did = True
            if wstate["next_dma"] < len(wpieces) and wstate["next_dma"] - wstate["next_cv"] < 2:
                j = wstate["next_dma"]
                dram, row0, col0, ncols, dst, scal, key, rd = wpieces[j]
                sap_, skey = stg(j, ncols)
                DMA("sp", sap_, dram[row0:row0 + 128, col0:col0 + ncols], [], [skey])
                wstate["next_dma"] += 1
                did = True
            if not flush or not did:
                break
            if wstate["next_cv"] == len(wpieces):
                break
    def units_for(d):
        nb = 32 // d
        return [(r, nq) for r in range(d) for nq in range(nb // 2, nb)]

    def vidx(d, r, n):
        if d == 1:
            return n - 15
        if d == 4:
            return 17 + (n - 3) * 4 + r
        return 37 + n * 16 + r

    pt_rr = [0]
    s_rr = [0]

    def make_contig(p):
        kk = [("KT", p, s) for s in range(8)]
        qq = [("QT", 2 * p + e_, s) for e_ in range(2) for s in range(4)]
        COPY("dve", rb16(o_kc4, 0, n=2560), AP(KT, 0, 128, p * 4096 + 1536, [[1, 4], [512, 5], [4, 128]]), kk, ["kc4"])
        COPY("dve", rb16(o_qc4, 0, n=2048), AP(QT, 0, 128, p * NT, [[1, 4], [512, 4], [4, 128]]), qq, ["qc4"])
        ACT(rb16(o_kc16, 0, n=4096), AP(KT, 0, 128, p * 4096, [[2048, 2], [1, 16], [16, 128]]), AF.Copy, kk, ["kc16"])
        COPY("dve", rb16(o_qc16, 0, n=2048), AP(QT, 0, 128, p * NT, [[1, 16], [16, 128]]), qq, ["qc16"])

    def head_attention(h, vbi):
        p = h // 2
        hr = 64 * (h % 2)
        zr = 64 - hr
        vo = o_vb[vbi]
        vcol = 0 if h % 2 == 0 else 64
        batches = []
        for d in (1, 4, 16):
            us = units_for(d)
            for i in range(0, len(us), 4):
                batches.append((d, us[i:i + 4]))
        kkeys = [("KT", p, s) for s in range(8)]
        qkeys = [("QT", h, s) for s in range(4)]
        started = set()

        OWN = [0, 2, 4, 7]
        PRV = [6, 1, 3, 5]

        def emit_qk(bi):
            d, us = batches[bi]
            sb_ = s_rr[0] % 2
            s_rr[0] += 1
            scol = sb_ * 1024

            def kblk(r, nk):
                if d == 4:
                    return rb16(o_kc4, (r * 5 + nk - 3) * 128, npart=64, p0=hr, n=128)
                if d == 16:
                    return rb16(o_kc16, (nk * 16 + r) * 128, npart=64, p0=hr, n=128)
                return AP(KT, hr, 64, p * 4096 + r + d * 128 * nk, [[d, 128]])

            def qblk(r, nq, n):
                if d == 4:
                    return rb16(o_qc4, (r * 4 + nq - 4) * 128, npart=64, p0=hr, n=n)
                if d == 16:
                    return rb16(o_qc16, r * 128, npart=64, p0=hr, n=n)
                return AP(QT, hr, 64, p * NT + r + d * 128 * nq - 2048, [[d, n]])
            ckeys = {1: [], 4: ["kc4", "qc4"], 16: ["kc16", "qc16"]}[d]

            def sslot(s_, n):
                return bass.AP(psB, scol + s_ * 128, [[2048, 128], [1, n]])
            lst = []
            if d < 16:
                r = us[0][0]
                nqs = [u[1] for u in us]
                for j in range(3):
                    lst.append((sslot(OWN[j], 256), kblk(r, nqs[j]), qblk(r, nqs[j], 256), True, True, False))
                lst.append((sslot(PRV[0], 128), kblk(r, nqs[0] - 1), qblk(r, nqs[0], 128), True, True, False))
                lst.append((sslot(OWN[3], 128), kblk(r, nqs[3]), qblk(r, nqs[3], 128), True, True, False))
            else:
                for j, (r, nq) in enumerate(us):
                    lst.append((sslot(PRV[j], 128), kblk(r, nq - 1), qblk(r, nq, 128), True, True, False))
                    lst.append((sslot(OWN[j], 128), kblk(r, nq), qblk(r, nq, 128), True, True, False))
            MM(lst, (kkeys + qkeys) if d == 1 else ckeys, [("bank", 4 + 2 * sb_), ("bank", 5 + 2 * sb_)])
            return sb_

        def emit_rest(bi, sb_):
            d, us = batches[bi]
            pi = pt_rr[0] % 3
            pt_rr[0] += 1
            scol = sb_ * 1024
            pt_ap = rb16(o_pt[pi], 0, n=1024)
            ACT(pt_ap, bass.AP(psB, scol, [[2048, 128], [1, 1024]]), AF.Exp,
                [("bank", 4 + 2 * sb_), ("bank", 5 + 2 * sb_)], [("pt", pi)])
            TT("dve", pt_ap, pt_ap, mask_ap, ALU.mult, [("pt", pi), "mask"], [("pt", pi)])
            lst = []
            vks = set()

            def vblk(r, nk):
                vks.add(vkey(vbi, d, r, nk))
                return rb16(vo, vidx(d, r, nk) * 192 + vcol, n=128)

            def ocol(r, nq, n):
                return bass.AP(psA, r + d * 128 * nq - 2048, [[2048, 128], [d, n]])

            def pslot(s_, n):
                return rb16(o_pt[pi], s_ * 128, n=n)

            def st_for(r, nq):
                g = (r + d * 128 * nq - 2048) // 512
                s_ = g not in started
                started.add(g)
                return s_
            if d == 1:
                r = us[0][0]
                nqs = [u[1] for u in us]
                for j in range(3):
                    lst.append((ocol(r, nqs[j], 256), vblk(r, nqs[j]), pslot(OWN[j], 256), st_for(r, nqs[j]), True, True))
                lst.append((ocol(r, nqs[0], 128), vblk(r, nqs[0] - 1), pslot(PRV[0], 128), False, True, True))
                lst.append((ocol(r, nqs[3], 128), vblk(r, nqs[3]), pslot(OWN[3], 128), False, True, True))
            elif d == 4:
                for j, (r, nq) in enumerate(us):
                    lst.append((ocol(r, nq, 128), vblk(r, nq - 1), pslot(PRV[j], 128), st_for(r, nq), True, True))
                    lst.append((ocol(r, nq, 128), vblk(r, nq), pslot(OWN[j], 128), False, True, True))
            else:
                for j, (r, nq) in enumerate(us):
                    for (s_, nk) in ((PRV[j], nq - 1), (OWN[j], nq)):
                        vap = vblk(r, nk)
                        for g in range(4):
                            lst.append((bass.AP(psA, 512 * g + r, [[2048, 128], [16, 32]]), vap,
                                        rb16(o_pt[pi], s_ * 128 + 32 * g, n=32), False, True, True))
            if d == 1:
                okeys_ = [("O", (us[0][1] - 16) // 4)]
            else:
                okeys_ = [("O", g_) for g_ in range(4)]
            MM(lst, [("pt", pi)] + sorted(vks, key=str), okeys_)

        sbs = {0: emit_qk(0)}
        for bi in range(len(batches)):
            if bi + 1 < len(batches):
                sbs[bi + 1] = emit_qk(bi + 1)
            emit_rest(bi, sbs[bi])
            if (bi + 12 * (h % 4)) % 3 == 2:
                pump_weights()
        lnz = rf32(o_rz, 0, npart=64, p0=zr, n=2048)
        ocn = rf32(o_oc, 0, npart=64, p0=hr, n=2048)
        for hf in range(2):
            c0_ = hf * 1024
            ok_ = [("O", 2 * hf), ("O", 2 * hf + 1)]
            ACT(rf32(o_rz, c0_, npart=64, p0=zr, n=1024), bass.AP(psA, zr * 2048 + c0_, [[2048, 64], [1, 1024]]), AF.Ln,
                ok_, [("rz", zr, hf)])
            COPY("dve", rf32(o_oc, c0_, npart=64, p0=hr, n=1024), bass.AP(psA, hr * 2048 + c0_, [[2048, 64], [1, 1024]]),
                 ok_, [("oc", hr, hf)])
        RZ_Z = [("rz", zr, 0), ("rz", zr, 1)]
        RZ_H = [("rz", hr, 0), ("rz", hr, 1)]
        OC_H = [("oc", hr, 0), ("oc", hr, 1)]
        if h < 7:
            ACT(lnz, lnz, AF.Exp, RZ_Z, RZ_Z, scale=-1.0)
            rzs = rf32(o_rz, 0, npart=64, p0=hr, n=2048)
            COPY("dve", rzs, lnz, RZ_Z, RZ_H)
            TT("dve", AP(QT, hr, 64, p * NT, [[1, 2048]]), ocn, rzs, ALU.mult, RZ_H + OC_H, [("QT", h, c_) for c_ in range(4)])
        else:
            for hf in range(2):
                c0_ = hf * 1024
                lz = rf32(o_rz, c0_, npart=64, p0=zr, n=1024)
                rs = rf32(o_rz, c0_, npart=64, p0=hr, n=1024)
                ACT(lz, lz, AF.Exp, [("rz", zr, hf)], [("rz", zr, hf)], scale=-1.0)
                COPY("dve", rs, lz, [("rz", zr, hf)], [("rz", hr, hf)])
                TT("dve", AP(QT, hr, 64, p * NT + c0_, [[1, 1024]]), rf32(o_oc, c0_, npart=64, p0=hr, n=1024), rs, ALU.mult,
                   [("rz", hr, hf), ("oc", hr, hf)], [("QT", h, 2 * hf), ("QT", h, 2 * hf + 1)])

    load_vblocks(0, 0)
    for p in range(4):
        if p + 1 < 4:
            load_vblocks(p + 1, (p + 1) % 2)
        make_contig(p)
        head_attention(2 * p, p % 2)
        head_attention(2 * p + 1, p % 2)
    cvu = Carve()
    cvu.off = p1a_end
    c2 = Ctx()
    hT0_ = cvu.take(8 * 512 * 2)
    c2.hb = [cvu.take(2048) for _ in range(2)]
    c2.junk = cvu.take(2048)
    o_wkv = cvu.take(8 * 512 * 2)
    o_bT = cvu.take(256 * 4)
    assert cvu.off <= o_mask
    o_vn = [cvu.take(512) for _ in range(2)]
    cv = Carve()
    c2.hT = [hT0_, cv.take(8 * 512 * 2)]
    c2.f = [cv.take(2048) for _ in range(6)]
    c2.b = [cv.take(1024) for _ in range(4)]
    _fr = [0]
    _br = [0]

    def _fpool():
        i = _fr[0]
        _fr[0] = (i + 1) % 6
        return i

    def _bpool():
        i = _br[0]
        _br[0] = (i + 1) % 4
        return i
    c2.fpool = _fpool
    c2.bpool = _bpool
    o_xr = [cv.take(4096) for _ in range(3)]
    xr_ctr = [0]
    o_xs = [cv.take(4096)]
    o_qm = [cv.take(2 * 512 * 2) for _ in range(2)]
    o_sgm = [cv.take(2 * 512 * 2) for _ in range(2)]
    o_sgg = [cv.take(2 * 512 * 4) for _ in range(2)]
    o_yg = [cv.take(2 * 512 * 2) for _ in range(2)]
    o_ym = [cv.take(2 * 512 * 2) for _ in range(2)]
    o_sqv = [cv.take(1024) for _ in range(2)]
    o_mkt = cv.take(2 * 256 * 2)
    o_mv = cv.take(2 * 2 * 192 * 2)
    assert cv.off <= p2_low_end, (cv.off, p2_low_end)
    cv.off = p2_low_end
    o_hmT = cv.take(8 * 256 * 2)
    assert cv.off <= p1a_end, cv.off
    VB0_KEYS = ([("vb", 0, "a", n0) for n0 in (15, 16, 20, 24, 28)] + [("vb", 0, "b", n) for n in range(3, 8)]
                + [("vb", 0, "c", n, r0) for n in range(2) for r0 in (0, 8)])

    def slab_1b_x(so, extra_r=()):
        hb_i = so % 2
        hTo = c2.hT[hb_i]
        DMA("sp", rb16(hTo, 0, n=4096), hscr[so * 128:(so + 1) * 128, :], [("hscr", so)] + list(extra_r),
            [("hT", hb_i, t) for t in range(4)])

    xstate["pool"] = list(xpool)
    P.op("sp", lambda e: e.nop(), [], VB0_KEYS + ["vb0free"], dur=50)
    slab_1b_x(0, extra_r=["vb0free"])
    P.default_prio = -40000.0
    DMA("sp", rf32(o_bT, 0, n=256), bT_d[:, :], ["vb0free"], ["bT"])
    for kc in range(8):
        load_weight_piece("sp", "dve", w_kv, kc * 128, 0, 512, rb16(o_wkv, kc * 512, n=512), cs(C_MNG + kc),
                          ("Wkv", kc), rd=["c_mng", "vb0free"])
    for t in range(2):
        x_tile(c2, mem_d, t * 128, rb16(o_hmT, t * 128, dims=[[256, 8], [1, 128]]), ("hmT", t), extra_r=["vb0free"], tbank=7)
    P.default_prio = 0.0
    pump_weights(flush=True)

    P.barrier()
    def preamble_1b():
        WKV = [("Wkv", kc) for kc in range(8)]
        HMT = [("hmT", 0), ("hmT", 1)]
        for c in range(2):
            b = proj_f2(lambda kc, c=c: rb16(o_wkv, kc * 512 + c * 128, n=128), WKV, o_hmT, 256, HMT, hstride=256)
            unit_norm(c2, b, 256, rb16(o_mkt, c * 256, n=256), cs(C_GMK), ["c_gmk"], [("mkt", c)])
        MSET("dve", rb16(o_mv, 0, n=768), 1.0, [], ["mv"])
        for t in range(2):
            b = next_bank()
            MM([(bank_ap(b, 0, 256), rb16(o_hmT, kc * 256 + t * 128, n=128), rb16(o_wkv, kc * 512 + 256, n=256), kc == 0, kc == 7, False)
                for kc in range(8)], WKV + HMT, [("bank", b)])
            ACT(rb16(o_mv, t * 384, dims=[[192, 2], [128, 2], [1, 64]]), bank_ap(b, 0, dims=[[128, 2], [64, 2], [1, 64]]), AF.Copy,
                [("bank", b), "mv"], ["mv"])

    WKV = [("Wkv", kc) for kc in range(8)]
    HMT = [("hmT", 0), ("hmT", 1)]
    W1B = [("W1b", kc, j) for kc in range(8) for j in range(2)]
    WOUT = [("Wout", kc) for kc in range(8)]

    def w1b_ap(c0):
        return lambda kc: AP(W1b, 0, 128, kc * 1792 + c0, [[1, 128]])

    def gate_chunk(c0, hTo, hkeys, out_ap, okeys):
        b = proj_f2(w1b_ap(c0), W1B, hTo, 512, hkeys)
        f1 = c2.fpool()
        f1_ap = rf32(c2.f[f1], 0, n=512)
        ACT(f1_ap, bank_ap(b, 0, 512), AF.Tanh, [("bank", b)], [("fp", f1)], scale=0.5)
        STT(out_ap, f1_ap, 1.0, bank_ap(b, 0, 512), ALU.add, ALU.mult, [("bank", b), ("fp", f1)], list(okeys))

    def slab_1b(so):
        hb_i = so % 2
        sl2 = so % 2
        hTo = c2.hT[hb_i]
        hkeys = [("hT", hb_i, t) for t in range(4)]
        if so > 0:
            slab_1b_x(so)
        for c in range(4):
            f2 = c2.fpool()
            f2_ap = rf32(c2.f[f2], 0, n=512)
            gate_chunk(768 + c * 128, hTo, hkeys, f2_ap, [("fp", f2)])
            qa = AP(QT, 0, 128, c * NT + so * SL, [[1, 512]])
            TT("dve", qa, qa, f2_ap, ALU.mult, [("fp", f2), ("QT", 2 * c, so), ("QT", 2 * c + 1, so)],
               [("QT", 2 * c, so), ("QT", 2 * c + 1, so)])
        for c in range(2):
            b = proj_f2(w1b_ap(1280 + c * 128), W1B, hTo, 512, hkeys)
            unit_norm(c2, b, 512, rb16(o_qm[sl2], c * 512, n=512), None, [], [("qm", sl2, c)])
            gate_chunk(1536 + c * 128, hTo, hkeys, rb16(o_sgm[sl2], c * 512, n=512), [("sgm", sl2, c)])
        for hm_ in range(4):
            cm = hm_ // 2
            hr = 64 * (hm_ % 2)
            zr = 64 - hr
            vcol = 0 if hm_ % 2 == 0 else 64
            pms = []
            for j in range(2):
                b = next_bank()
                MM([(bank_ap(b, 0, 512), rb16(o_mkt, cm * 256 + j * 128, npart=64, p0=hr, n=128),
                     rb16(o_qm[sl2], cm * 512, npart=64, p0=hr, n=512), True, True, False)],
                   [("mkt", cm), ("qm", sl2, cm)], [("bank", b)])
                pm = c2.bpool()
                ACT(rb16(c2.b[pm], 0, n=512), bank_ap(b, 0, 512), AF.Exp, [("bank", b)], [("bp", pm)])
                pms.append(pm)
            bo = next_bank()
            MM([(bank_ap(bo, 0, 512), rb16(o_mv, j * 384 + cm * 192 + vcol, n=128), rb16(c2.b[pms[j]], 0, n=512), j == 0, j == 1, False)
                for j in range(2)], ["mv", ("bp", pms[0]), ("bp", pms[1])], [("bank", bo)])
            f1 = c2.fpool()
            f1z = rf32(c2.f[f1], 0, npart=64, p0=zr, n=512)
            f1h = rf32(c2.f[f1], 0, npart=64, p0=hr, n=512)
            ACT(f1z, bank_ap(bo, 0, 512, p0=zr, npart=64), AF.Ln, [("bank", bo)], [("fp", f1)])
            ACT(f1z, f1z, AF.Exp, [("fp", f1)], [("fp", f1)], scale=-1.0)
            COPY("dve", f1h, f1z, [("fp", f1)], [("fp", f1)])
            f2 = c2.fpool()
            f2_ap = rf32(c2.f[f2], 0, npart=64, p0=hr, n=512)
            TT("dve", f2_ap, bank_ap(bo, 0, 512, p0=hr, npart=64), f1h, ALU.mult, [("bank", bo), ("fp", f1)], [("fp", f2)])
            TT("dve", rb16(o_ym[sl2], cm * 512, npart=64, p0=hr, n=512), f2_ap, rb16(o_sgm[sl2], cm * 512, npart=64, p0=hr, n=512),
               ALU.mult, [("fp", f2), ("sgm", sl2, cm)], [("ym", sl2, hm_)])
        spb = [next_bank(), next_bank()]
        held.update(spb)
        for t in range(4):
            b = next_bank()
            MM([(bank_ap(b, 0, 256), rb16(hTo, kc * 512 + t * 128, n=128), AP(W1b, 0, 128, kc * 1792 + 256, [[1, 256]]), kc == 0, kc == 7, False)
                for kc in range(8)], W1B + [("hT", hb_i, t)], [("bank", b)])
            vi = t % 2
            ACT(rf32(o_sqv[vi], 0, n=256), bank_ap(b, 0, 256), AF.Square, [("bank", b)], [("sqv", vi)])
            tt = tile_ctr[0]
            tile_ctr[0] += 1
            g0 = (tt % 8) * 12
            P.op("dve", lambda e, vi=vi, g0=g0: e.tensor_reduce(out=gst[:, g0:g0 + 4], in_=rf32(o_sqv[vi], 0, dims=[[64, 4], [1, 64]]),
                                                                axis=AX.X, op=ALU.add),
                 [("sqv", vi)], [("gst", tt % 8, 0)])
            ACT(gst[:, g0 + 4:g0 + 8], gst[:, g0:g0 + 4], AF.Ln, [("gst", tt % 8, 0)], [("gst", tt % 8, 1)], scale=1.0 / 64, bias=EPS)
            ACT(gst[:, g0 + 8:g0 + 12], gst[:, g0 + 4:g0 + 8], AF.Exp, [("gst", tt % 8, 1)], [("gst", tt % 8, 2)], scale=-0.5)
            TT("dve", rb16(o_vn[vi], 0, dims=[[64, 4], [1, 64]]), bank_ap(b, 0, dims=[[64, 4], [1, 64]]),
               bass.AP(gst, g0 + 8, [[96, 128], [1, 4], [0, 64]]), ALU.mult, [("bank", b), ("gst", tt % 8, 2)], [("vn", vi)])
            MM([(bank_ap(spb[hh // 2], t * 128, 128, p0=64 * (hh % 2), npart=64), rb16(o_vn[vi], hh * 64, n=64),
                 wsTb[:, hh * 128:(hh + 1) * 128], True, True, True) for hh in range(4)],
               [("vn", vi), "wsT"], [("bank", spb[0]), ("bank", spb[1])])
        for c in range(2):
            gate_chunk(512 + c * 128, hTo, hkeys, rf32(o_sgg[sl2], c * 512, n=512), [("sgg", sl2, c)])
        for pp in range(2):
            gb = proj_f2(w1b_ap(pp * 128), W1B, hTo, 512, hkeys)
            fa = c2.fpool()
            STT(rf32(c2.f[fa], 0, dims=[[128, 4], [1, 128]]), bank_ap(spb[pp], 0, dims=[[128, 4], [1, 128]]), cs(C_VG + pp),
                rf32(o_bT, pp * 128, dims=[[0, 4], [1, 128]]), ALU.mult, ALU.add, [("bank", spb[pp]), "bT", "c_vg"], [("fp", fa)])
            fb = c2.fpool()
            TT("dve", rf32(c2.f[fb], 0, n=512), bank_ap(gb, 0, 512), rf32(c2.f[fa], 0, n=512), ALU.mult,
               [("bank", gb), ("fp", fa)], [("fp", fb)])
            TT("dve", rb16(o_yg[sl2], pp * 512, n=512), rf32(c2.f[fb], 0, n=512), rf32(o_sgg[sl2], pp * 512, n=512), ALU.mult,
               [("fp", fb), ("sgg", sl2, pp)], [("yg", sl2, pp)])
        held.clear()
        ykeys_g = [("yg", sl2, 0), ("yg", sl2, 1)]
        ykeys_am = [("QT", h, so) for h in range(8)] + [("ym", sl2, h) for h in range(4)]
        for t in range(4):
            ri = xr_ctr[0] % 3
            xr_ctr[0] += 1
            row0 = so * SL + t * 128
            xr_full = rf32(o_xr[ri], 0, n=1024)
            DMA("sp", xr_full, xo[row0:row0 + 128, :], [], [("xr", ri, 0), ("xr", ri, 1)])
            for half in range(2):
                b = next_bank()
                def lt_of(c):
                    if c < 2:
                        return rb16(o_yg[sl2], c * 512 + t * 128, n=128)
                    if c < 6:
                        return AP(QT, 0, 128, (c - 2) * NT + so * SL + t * 128, [[1, 128]])
                    return rb16(o_ym[sl2], (c - 6) * 512 + t * 128, n=128)
                MM([(bank_ap(b, 0, 512), lt_of(c), AP(Wout, 0, 128, c * 1024 + half * 512, [[1, 512]]), c == 2, False, False)
                    for c in (2, 3, 4, 5, 6, 7)], ykeys_am + WOUT, [("bank", b)])
                MM([(bank_ap(b, 0, 512), lt_of(c), AP(Wout, 0, 128, c * 1024 + half * 512, [[1, 512]]), False, c == 1, False)
                    for c in (0, 1)], ykeys_g + WOUT, [("bank", b)])
                xh_ = rf32(o_xr[ri], half * 512, n=512)
                TT("dve", xh_, bank_ap(b, 0, 512), xh_, ALU.add, [("bank", b), ("xr", ri, half)], [("xr", ri, half)])
            DMA("sp", out_d[row0:row0 + 128, :], xr_full, [("xr", ri, 0), ("xr", ri, 1)], [("out", row0)])

    xstate["pool"] = list(xpool) + [((lambda n, o_=o_: rf32(o_, 0, n=n)), ("xs", k_)) for k_, o_ in enumerate(o_xs)]
    nbanks[0] = 8
    P.default_prio = -40000.0
    preamble_1b()
    P.default_prio = 0.0
    for so in range(4):
        slab_1b(so)
    P.op("sp", lambda e: e.nop(), [("out", so * SL + t * 128) for so in range(4) for t in range(4)], [])
    P.emit()
    return nc


_NC_CACHE = {}


def kernel(x, mem, norm_gain, w_in, gmlp_v_gain, gmlp_w_s, gmlp_b, attn_q_gain, attn_k_gain,
           mem_norm_gain, w_mem_kv, mem_q_gain, mem_k_gain, w_out):
    f = np.float32
    x = np.asarray(x, f)
    mem = np.asarray(mem, f)
    B, S, D = x.shape
    w_in0 = np.ascontiguousarray(np.asarray(w_in, f)[0])
    w_out0 = np.ascontiguousarray(np.asarray(w_out, f)[0])
    w_kv0 = np.ascontiguousarray(np.asarray(w_mem_kv, f)[0])
    ng = np.ascontiguousarray(np.asarray(norm_gain, f)[0].reshape(8, 128).T)
    mng = np.ascontiguousarray(np.asarray(mem_norm_gain, f)[0].reshape(8, 128).T)
    vgn = np.asarray(gmlp_v_gain, f)[0]
    vg = np.ascontiguousarray(vgn.reshape(2, 128).T)
    ws = np.asarray(gmlp_w_s, f)[0]
    wsT = np.ascontiguousarray(ws.transpose(2, 0, 1).reshape(128, 512))
    bb = np.asarray(gmlp_b, f)[0]
    bT = np.ascontiguousarray(np.repeat(bb.reshape(2, 2, 1, 128), 64, axis=2).reshape(2, 128, 128).transpose(1, 0, 2).reshape(128, 256))
    ag = np.ascontiguousarray(np.stack([np.tile(np.asarray(attn_q_gain, f)[0], 2), np.tile(np.asarray(attn_k_gain, f)[0], 2)], 1))
    mg = np.ascontiguousarray(np.stack([np.tile(np.asarray(mem_q_gain, f)[0], 2), np.tile(np.asarray(mem_k_gain, f)[0], 2)], 1))
    ii = np.arange(128)
    ident = (ii[:, None] == ii[None, :]).astype(f)
    blkc = ((ii[:, None] // 64) == (ii[None, :] // 64)).astype(f) / 64.0
    cid = np.ascontiguousarray(np.concatenate([ident, blkc], axis=1))
    own = (ii[None, :] >= ii[:, None]).astype(f)
    prv = (ii[:, None] >= ii[None, :]).astype(f)
    cmask = np.ascontiguousarray(np.concatenate([own, prv, own, prv, own, prv, prv, own], axis=1))
    ctril = np.ascontiguousarray(np.tile(own, (1, 4)))
    if "nc" not in _NC_CACHE:
        _NC_CACHE["nc"] = build_nc()
    nc = _NC_CACHE["nc"]
    in_maps = []
    for c in range(8):
        b, half = c // 2, c % 2
        xo = np.ascontiguousarray(x[b, half * NT:(half + 1) * NT])
        if half == 0:
            xh = np.zeros((NT, D), f)
            hmv = np.zeros((128, 1), f)
        else:
            xh = np.ascontiguousarray(x[b, 0:NT])
            hmv = np.ones((128, 1), f)
        in_maps.append(dict(xo=xo, xh=xh, hm=hmv, mem=np.ascontiguousarray(mem[b]), w_in=w_in0, w_out=w_out0,
                            w_kv=w_kv0, ng=ng, mng=mng, vg=vg, wsT=wsT, bT=bT, ag=ag, mg=mg,
                            cid=cid, cmask=cmask, ctril=ctril))
    res = run_bass_kernel_spmd(nc, in_maps, core_ids=list(range(8)))
    out = np.empty((B, S, D), f)
    for c in range(8):
        b, half = c // 2, c % 2
        out[b, half * NT:(half + 1) * NT] = res.results[c]["out"]
    return out
```

```python
import numpy as np
import concourse.bass as bass
import concourse.mybir as mybir
from concourse.bass_utils import run_bass_kernel_spmd

F32 = mybir.dt.float32
BF16 = mybir.dt.bfloat16
AF = mybir.ActivationFunctionType
ALU = mybir.AluOpType
AX = mybir.AxisListType

NT = 2048
SL = 512
EPS = 1e-6
GU, GV, GG, AQ, AK, AV, AGT, MQ, MG = 0, 256, 512, 768, 1280, 1792, 2304, 2816, 3072


class Prog:
    ENG = ("pe", "act", "dve", "pool", "sp")

    def __init__(self, nc):
        self.nc = nc
        self.ops = []
        self.last_w = {}
        self.readers = {}
        self.ndma_sems = 16
        self.bar_from = 0
        self.default_prio = 0.0

    def op(self, eng, fn, reads=(), writes=(), dma=False, dur=None, lat=None, prio=0.0, tbl=None):
        i = len(self.ops)
        deps = set()
        for k in reads:
            w = self.last_w.get(k)
            if w is not None:
                deps.add(w)
        for k in writes:
            w = self.last_w.get(k)
            if w is not None:
                deps.add(w)
            for r in self.readers.get(k, ()):
                deps.add(r)
        deps.discard(i)
        self.ops.append(dict(eng=eng, fn=fn, deps=deps, dma=dma, dur=dur, lat=lat, fence=False,
                             prio=(prio if prio else self.default_prio), wr=list(writes), tbl=tbl))
        for k in reads:
            self.readers.setdefault(k, []).append(i)
        for k in writes:
            self.last_w[k] = i
            self.readers[k] = []
        return i

    def barrier(self):
        deps = set()
        last = {}
        for i in range(self.bar_from, len(self.ops)):
            o = self.ops[i]
            if o["dma"]:
                deps.add(i)
            else:
                last[o["eng"]] = i
        deps |= set(last.values())
        first = None
        for e in self.ENG:
            i = len(self.ops)
            d = set(deps) if first is None else {first}
            self.ops.append(dict(eng=e, fn=(lambda eng: eng.nop()), deps=d, dma=False, dur=50, lat=None, fence=True, prio=0.0, wr=["FENCE"], tbl=None))
            if first is None:
                first = i
        self.bar_from = len(self.ops)

    def schedule(self):
        ops = self.ops
        DEF = dict(pe=300, act=700, dve=700, pool=1500, sp=150)
        order = []
        seg = []
        segs = []
        for i, o in enumerate(ops):
            if o["fence"]:
                if seg:
                    segs.append(("ops", seg))
                    seg = []
                segs.append(("fence", [i]))
            else:
                seg.append(i)
        if seg:
            segs.append(("ops", seg))
        prev_seg_order = []
        for kind, seg in segs:
            if kind == "fence":
                fi = seg[0]
                last = {}
                for j in prev_seg_order:
                    if not ops[j]["dma"]:
                        last[ops[j]["eng"]] = j
                if ops[fi]["deps"] and len(ops[fi]["deps"]) > 1:
                    ops[fi]["deps"] |= set(last.values())
                order.extend(seg)
                continue
            seg_start = len(order)
            inseg = set(seg)
            ndeps = {}
            users = {}
            for i in seg:
                ds = [d for d in ops[i]["deps"] if d in inseg]
                ndeps[i] = len(ds)
                for d in ds:
                    users.setdefault(d, []).append(i)
            def dur_of(o):
                if o["dma"]:
                    return o["lat"] if o["lat"] is not None else 4000.0
                return o["dur"] if o["dur"] is not None else DEF[o["eng"]]
            blev = {}
            for i in reversed(seg):
                m = 0.0
                for u in users.get(i, ()):
                    if blev[u] > m:
                        m = blev[u]
                blev[i] = m + dur_of(ops[i]) + 500.0 + ops[i]["prio"]
            ready = [i for i in seg if ndeps[i] == 0]
            est = {}
            for i in ready:
                est[i] = 0.0
            fin = {}
            etime = {e: 0.0 for e in self.ENG}
            dma_free = [0.0]
            act_tbl = [6]
            nleft = len(seg)
            while nleft:
                best = None
                bkey = None
                for i in ready:
                    o = ops[i]
                    et = etime[o["eng"]]
                    if o["tbl"] is not None and o["tbl"] != act_tbl[0]:
                        et = et + 1300.0
                    if est[i] <= et:
                        key = (et, 0, -blev[i], i)
                    else:
                        key = (est[i], 1, -blev[i], i)
                    if bkey is None or key < bkey:
                        bkey = key
                        best = i
                i = best
                ready.remove(i)
                o = ops[i]
                st = bkey[0]
                dur = o["dur"] if o["dur"] is not None else DEF[o["eng"]]
                if o["dma"]:
                    occ = 350.0 if o["eng"] == "act" else (600.0 if o["eng"] == "pool" else 150.0)
                    etime[o["eng"]] = st + occ
                    xfer = (o["lat"] if o["lat"] is not None else 4000.0) - 2500.0
                    x0 = max(st + occ, dma_free[0])
                    dma_free[0] = x0 + xfer
                    fin[i] = dma_free[0] + 2000.0
                else:
                    etime[o["eng"]] = st + dur
                    fin[i] = st + dur
                if o["tbl"] is not None:
                    act_tbl[0] = o["tbl"]
                order.append(i)
                nleft -= 1
                for u in users.get(i, ()):
                    ndeps[u] -= 1
                    lat = 100.0 if (ops[u]["eng"] == o["eng"] and not o["dma"]) else 500.0
                    t = fin[i] + lat
                    if est.get(u, 0.0) < t:
                        est[u] = t
                    if ndeps[u] == 0:
                        ready.append(u)
            prev_seg_order = order[seg_start:]
        assert len(order) == len(ops)
        return order

    def _check_deadlock(self, ops, per_eng, dsem, sem):
        val = {}
        pos = {e: 0 for e in self.ENG}
        total = sum(len(v) for v in per_eng.values())
        done = 0
        progress = True
        while progress and done < total:
            progress = False
            for e in self.ENG:
                while pos[e] < len(per_eng[e]):
                    i = per_eng[e][pos[e]]
                    o = ops[i]
                    ok = True
                    for d in o["deps"]:
                        po = ops[d]
                        if po["sig"] is None:
                            continue
                        if po["eng"] == "pe" and e == "pe" and not po["dma"]:
                            continue
                        s, v = po["sig"]
                        if val.get(id(s), 0) < v:
                            ok = False
                            break
                    if ok and o["dma"]:
                        j = o["dma_idx"]
                        if j >= self.ndma_sems:
                            s = dsem[e][j % self.ndma_sems]
                            if val.get(id(s), 0) < 16 * (j // self.ndma_sems):
                                ok = False
                    if not ok:
                        break
                    if o["sig"] is not None:
                        s, v = o["sig"]
                        val[id(s)] = val.get(id(s), 0) + (16 if o["dma"] else 1)
                        assert val[id(s)] == v or o["dma"], (i, val[id(s)], v)
                    pos[e] += 1
                    done += 1
                    progress = True
        if done < total:
            stuck = {e: (per_eng[e][pos[e]] if pos[e] < len(per_eng[e]) else None) for e in self.ENG}
            raise RuntimeError("semaphore deadlock; stuck ops: %r" % (stuck,))

    def emit(self, reorder=True):
        nc = self.nc
        ops = self.ops
        order = self.schedule() if reorder else list(range(len(ops)))
        needed = set()
        for i, o in enumerate(ops):
            for d in o["deps"]:
                po = ops[d]
                if po["eng"] == "pe" and o["eng"] == "pe" and not po["dma"]:
                    continue
                needed.add(d)
        sem = {e: nc.alloc_semaphore("s_" + e) for e in ("pe", "act", "dve", "pool", "sp")}
        dsem = {}
        for e in ("sp", "pool", "act"):
            dsem[e] = [nc.alloc_semaphore("d_%s%d" % (e, j)) for j in range(self.ndma_sems)]
        cnt = {e: 0 for e in sem}
        dcnt = {e: 0 for e in dsem}
        for i in order:
            o = ops[i]
            if o["dma"]:
                e = o["eng"]
                j = dcnt[e]
                dcnt[e] += 1
                o["sig"] = (dsem[e][j % self.ndma_sems], 16 * (j // self.ndma_sems + 1))
                o["dma_idx"] = j
            elif i in needed:
                cnt[o["eng"]] += 1
                o["sig"] = (sem[o["eng"]], cnt[o["eng"]])
            else:
                o["sig"] = None
        per_eng = {e: [] for e in self.ENG}
        for i in order:
            per_eng[ops[i]["eng"]].append(i)

        self._check_deadlock(ops, per_eng, dsem, sem)
        import os as _os
        if _os.environ.get("KDEBUG"):
            for e in self.ENG:
                print("ENGINE", e, len(per_eng[e]))
                for i in per_eng[e][:int(_os.environ.get("KDEBUG"))]:
                    print("   ", i, ops[i]["wr"][:3], "dma" if ops[i]["dma"] else "")

        def run_engine(ename, eng):
            waited = {}
            for i in per_eng[ename]:
                o = ops[i]
                waits = {}
                for d in o["deps"]:
                    po = ops[d]
                    if po["sig"] is None:
                        continue
                    if po["eng"] == "pe" and ename == "pe" and not po["dma"]:
                        continue
                    s, v = po["sig"]
                    key = id(s)
                    if waits.get(key, (None, -1))[1] < v:
                        waits[key] = (s, v)
                if o["dma"]:
                    j = o["dma_idx"]
                    if j >= self.ndma_sems:
                        s = dsem[ename][j % self.ndma_sems]
                        v = 16 * (j // self.ndma_sems)
                        key = id(s)
                        if waits.get(key, (None, -1))[1] < v:
                            waits[key] = (s, v)
                for key, (s, v) in waits.items():
                    if waited.get(key, -1) >= v:
                        continue
                    eng.wait_ge(s, v)
                    waited[key] = v
                ins = o["fn"](eng)
                if o["sig"] is not None:
                    s, v = o["sig"]
                    ins.then_inc(s, 16 if o["dma"] else 1)

        with nc.Block() as block:
            @block.tensor
            def _(e):
                run_engine("pe", e)

            @block.scalar
            def _(e):
                run_engine("act", e)

            @block.vector
            def _(e):
                run_engine("dve", e)

            @block.gpsimd
            def _(e):
                run_engine("pool", e)

            @block.sync
            def _(e):
                run_engine("sp", e)


def build_nc():
    nc = bass.Bass("TRN2", target_bir_lowering=False)
    P = Prog(nc)

    def din(name, shape, dt=F32):
        return nc.dram_tensor(name, list(shape), dt, kind="ExternalInput")

    xo = din("xo", [NT, 1024])
    xh = din("xh", [NT, 1024])
    hm_d = din("hm", [128, 1])
    mem_d = din("mem", [256, 1024])
    w_in = din("w_in", [1024, 3328])
    w_out = din("w_out", [1024, 1024])
    w_kv = din("w_kv", [1024, 512])
    ng_d = din("ng", [128, 8])
    mng_d = din("mng", [128, 8])
    vg_d = din("vg", [128, 2])
    wsT_d = din("wsT", [128, 512])
    bT_d = din("bT", [128, 256])
    ag_d = din("ag", [128, 2])
    mg_d = din("mg", [128, 2])
    cid_d = din("cid", [128, 256])
    cmask_d = din("cmask", [128, 1024])
    ctril_d = din("ctril", [128, 512])
    out_d = nc.dram_tensor("out", [NT, 1024], F32, kind="ExternalOutput")
    scr = nc.dram_tensor("scr", [4096, 768], BF16, kind="Internal")
    hscr = nc.dram_tensor("hscr", [4 * 128, 4096], BF16, kind="Internal")

    def sb(name, cols, dt):
        return nc.alloc_sbuf_tensor(name, [128, cols], dt)

    def AP(t, p0, npart, col, dims):
        Fr = t.shape[1]
        return bass.AP(t, p0 * Fr + col, [[Fr, npart]] + [list(d) for d in dims])

    def fsz(ap):
        n = 1
        for s_ in ap.shape[1:]:
            n *= s_
        return n

    def ACT(out, in_, func, reads, writes, **kw):
        tbl = 0 if func == AF.Tanh else (6 if func == AF.Ln else None)
        P.op("act", lambda e: e.activation(out=out, in_=in_, func=func, **kw), reads, writes, dur=230 + 0.84 * fsz(in_), tbl=tbl)

    def TS(eng, out, in0, s1, s2, op0, op1, reads, writes):
        if op1 is None:
            P.op(eng, lambda e: e.tensor_scalar(out=out, in0=in0, scalar1=s1, scalar2=None, op0=op0), reads, writes, dur=120 + 0.6 * fsz(in0))
        else:
            P.op(eng, lambda e: e.tensor_scalar(out=out, in0=in0, scalar1=s1, scalar2=s2, op0=op0, op1=op1), reads, writes, dur=120 + 0.6 * fsz(in0))

    def TT(eng, out, in0, in1, op, reads, writes):
        P.op(eng, lambda e: e.tensor_tensor(out=out, in0=in0, in1=in1, op=op), reads, writes, dur=120 + 1.05 * fsz(in0))

    def STT(out, in0, scalar, in1, op0, op1, reads, writes):
        P.op("dve", lambda e: e.scalar_tensor_tensor(out=out, in0=in0, scalar=scalar, in1=in1, op0=op0, op1=op1), reads, writes, dur=120 + 1.05 * fsz(in0))

    def COPY(eng, out, in_, reads, writes):
        P.op(eng, lambda e: e.tensor_copy(out=out, in_=in_), reads, writes, dur=120 + 0.6 * fsz(in_))

    def MSET(eng, ap, val, reads, writes):
        P.op(eng, lambda e: e.memset(ap, val), reads, writes, dur=300 + 0.9 * fsz(ap))

    def MM(lst, reads, writes):
        lst = list(lst)

        def f(e):
            ins = None
            for (o, l, r, st, sp, sk) in lst:
                if sk:
                    ins = e.matmul(o, lhsT=l, rhs=r, start=st, stop=sp, skip_group_check=True)
                else:
                    ins = e.matmul(o, lhsT=l, rhs=r, start=st, stop=sp)
            return ins
        d_ = 0.0
        for (o, l, r, st, sp, sk) in lst:
            n_ = fsz(r)
            d_ += max(55 + 0.43 * n_, 150.0) if n_ > 32 else 75.0
        P.op("pe", f, reads, writes, dur=d_)

    def TR(lst, reads, writes):
        lst = list(lst)

        def f(e):
            ins = None
            for (o, i_) in lst:
                ins = e.transpose(out=o, in_=i_, identity=ident[:, :])
            return ins
        P.op("pe", f, list(reads) + ["ident"], writes, dur=130.0 * len(lst))

    def DMA(q, out, in_, reads, writes, prio=0.0):
        nbytes = fsz(out) * out.shape[0] * (2 if out.dtype == BF16 else 4)
        P.op(q, lambda e: e.dma_start(out=out, in_=in_), reads, writes, dma=True, lat=2500.0 + nbytes / 150.0, prio=prio)

    def ASEL(out, pattern, cm, reads, writes):
        P.op("pool", lambda e: e.affine_select(out=out, in_=out, pattern=pattern, compare_op=ALU.is_ge, fill=0.0,
                                               base=0, channel_multiplier=cm), reads, writes)

    KT = sb("KT", 4 * 4096, BF16)
    QT = sb("QT", 4 * NT, BF16)
    Wout = sb("Wout", 8 * 1024, BF16)
    W1b = sb("W1b", 8 * 1792, BF16)
    xbuf = [sb("xbuf%d" % i, 1024, F32) for i in range(2)]
    ident = sb("ident", 128, BF16)
    blk = sb("blk", 128, BF16)
    cst = sb("cst", 64, F32)
    stats = sb("stats", 3 * 80, F32)
    gst = sb("gst", 96, F32)
    C_HM, C_NG, C_MNG, C_VG, C_AG, C_MG, C_GQ, C_GMK = 0, 1, 9, 17, 19, 21, 23, 24

    def cs(c, n=1):
        return cst[:, c:c + n]

    RB = 104 * 1024
    R = nc.alloc_sbuf_tensor("R", [128, RB // 2], BF16)
    Rf = R.bitcast(F32)

    class Carve:
        def __init__(self):
            self.off = 0

        def take(self, nbytes):
            o = self.off
            self.off += (nbytes + 63) // 64 * 64
            assert self.off <= RB, self.off
            return o

    def rb16(off_bytes, col=0, npart=128, p0=0, dims=None, n=None):
        base = off_bytes // 2 + col
        if dims is None:
            dims = [[1, n]]
        return bass.AP(R, p0 * (RB // 2) + base, [[RB // 2, npart]] + [list(d) for d in dims])

    def rf32(off_bytes, col=0, npart=128, p0=0, dims=None, n=None):
        base = off_bytes // 4 + col
        if dims is None:
            dims = [[1, n]]
        return bass.AP(Rf, p0 * (RB // 4) + base, [[RB // 4, npart]] + [list(d) for d in dims])

    psA = nc.alloc_psum_tensor("psA", [128, 2048], F32)
    psB = nc.alloc_psum_tensor("psB", [128, 2048], F32)
    psBb = psB.bitcast(BF16)

    def bank_ap(i, col=0, n=512, p0=0, npart=128, dims=None):
        t = psA if i < 4 else psB
        c = (i % 4) * 512 + col
        if dims is None:
            dims = [[1, n]]
        return bass.AP(t, p0 * 2048 + c, [[2048, npart]] + [list(d) for d in dims])

    bank_rr = [0]
    nbanks = [8]
    held = set()

    def next_bank():
        while True:
            b = bank_rr[0]
            bank_rr[0] = (b + 1) % nbanks[0]
            if b not in held:
                return b

    psAb = psA.bitcast(BF16)

    def pT_ap(bk, col, n):
        tb = psAb if bk < 4 else psBb
        return bass.AP(tb, (bk % 4) * 1024 + col, [[4096, 128], [1, n]])

    def pT_all(bk):
        tb = psAb if bk < 4 else psBb
        return bass.AP(tb, (bk % 4) * 1024, [[4096, 128], [128, 8], [1, 128]])

    DMA("sp", cs(C_HM), hm_d[:, :], [], ["c_hm"])
    DMA("sp", cs(C_NG, 8), ng_d[:, :], [], ["c_ng"])
    DMA("sp", cs(C_MNG, 8), mng_d[:, :], [], ["c_mng"])
    DMA("sp", cs(C_VG, 2), vg_d[:, :], [], ["c_vg"])
    DMA("sp", cs(C_AG, 2), ag_d[:, :], [], ["c_ag"])
    DMA("sp", cs(C_MG, 2), mg_d[:, :], [], ["c_mg"])
    TS("dve", cs(C_GQ), cs(C_AG), cs(C_AG + 1), None, ALU.mult, None, ["c_ag"], ["c_gq"])
    TS("dve", cs(C_GQ), cs(C_GQ), 0.125, None, ALU.mult, None, ["c_gq"], ["c_gq"])
    TS("dve", cs(C_GMK), cs(C_MG), cs(C_MG + 1), None, ALU.mult, None, ["c_mg"], ["c_gmk"])
    TS("dve", cs(C_GMK), cs(C_GMK), 0.125, None, ALU.mult, None, ["c_gmk"], ["c_gmk"])
    idf = xbuf[1]
    DMA("sp", idf[:, 0:256], cid_d[:, :], [], [("x", 1)])
    ACT(ident[:, :], idf[:, 0:128], AF.Copy, [("x", 1)], ["ident"])
    ACT(blk[:, :], idf[:, 128:256], AF.Copy, [("x", 1)], ["blk"])
    KTf = KT.bitcast(F32)
    P.op("act", lambda e: e.memzero(KTf[:, :]), [], ["KTz"], dur=7000.0)
    wsTb = sb("wsTb", 512, BF16)

    wq = [0]
    tile_ctr = [0]

    xpool = [(lambda n, t_=xbuf[0]: t_[:, 0:n], ("x", 0)), (lambda n, t_=xbuf[1]: t_[:, 0:n], ("x", 1))]
    xstate = dict(pool=list(xpool))

    def xs_alloc():
        pl = xstate["pool"]
        i = wq[0] % len(pl)
        wq[0] += 1
        return pl[i]

    def load_weight_piece(q, ceng, dram, row0, col0, ncols, dst_ap, scal, key, rd=(), prio=0.0):
        xf, xk = xs_alloc()
        DMA(q, xf(ncols), dram[row0:row0 + 128, col0:col0 + ncols], [], [xk], prio=prio)
        TS(ceng, dst_ap, xf(ncols), scal, None, ALU.mult, None, [xk] + list(rd), [key])

    class Ctx:
        pass

    def x_tile(cx, src, row0, hT_out_ap, ktag, prio=0.0, extra_r=(), tbank=None):
        t = tile_ctr[0]
        tile_ctr[0] += 1
        xf, xk = xs_alloc()
        xfull = xf(1024)
        sc = (t % 80) * 3
        hbi = t % 2
        hb_ap = rb16(cx.hb[hbi], 0, n=1024)
        DMA("sp", xfull, src[row0:row0 + 128, :], list(extra_r), [xk], prio=prio)
        ACT(rb16(cx.junk, 0, n=1024), xfull, AF.Square, [xk], [("st", t, 0), "junk"], accum_out=stats[:, sc:sc + 1])
        ACT(stats[:, sc + 1:sc + 2], stats[:, sc:sc + 1], AF.Ln, [("st", t, 0)], [("st", t, 1)], scale=1.0 / 1024, bias=EPS)
        ACT(stats[:, sc + 2:sc + 3], stats[:, sc + 1:sc + 2], AF.Exp, [("st", t, 1)], [("st", t, 2)], scale=-0.5)
        TS("dve", hb_ap, xfull, stats[:, sc + 2:sc + 3], None, ALU.mult, None, [xk, ("st", t, 2)], [("hb", hbi)])
        bk = next_bank() if tbank is None else tbank
        TR([(pT_ap(bk, k * 128, 128), rb16(cx.hb[hbi], k * 128, n=128)) for k in range(8)], [("hb", hbi)], [("bank", bk)])
        COPY("dve", hT_out_ap, pT_all(bk), [("bank", bk)], [ktag])

    def proj_f2(wap_fn, wkeys, hTo, ncols, hkeys, hstride=512):
        b = next_bank()
        MM([(bank_ap(b, 0, ncols), wap_fn(kc), rb16(hTo, kc * hstride, n=ncols), kc == 0, kc == 7, False) for kc in range(8)],
           list(wkeys) + list(hkeys), [("bank", b)])
        return b

    def unit_norm(cx, b, ncols, out_ap, gain_ap, gkeys, okeys):
        sq = cx.bpool()
        sq_ap = rb16(cx.b[sq], 0, n=ncols)
        ACT(sq_ap, bank_ap(b, 0, ncols), AF.Square, [("bank", b)], [("bp", sq)])
        b2 = next_bank()
        MM([(bank_ap(b2, 0, ncols), blk[:, :], sq_ap, True, True, False)], [("bp", sq), "blk"], [("bank", b2)])
        f1 = cx.fpool()
        f1_ap = rf32(cx.f[f1], 0, n=ncols)
        ACT(f1_ap, bank_ap(b2, 0, ncols), AF.Ln, [("bank", b2)], [("fp", f1)], bias=EPS)
        f2 = cx.fpool()
        f2_ap = rf32(cx.f[f2], 0, n=ncols)
        ACT(f2_ap, f1_ap, AF.Exp, [("fp", f1)], [("fp", f2)], scale=-0.5)
        sc = gain_ap if gain_ap is not None else 1.0
        STT(out_ap, bank_ap(b, 0, ncols), sc, f2_ap, ALU.mult, ALU.mult, [("bank", b), ("fp", f2)] + list(gkeys), list(okeys))

    def make_ctx(cv, nf, nb):
        cx = Ctx()
        cx.hT = [cv.take(8 * 512 * 2) for _ in range(2)]
        cx.f = [cv.take(2048) for _ in range(nf)]
        cx.b = [cv.take(1024) for _ in range(nb)]
        cx.hb = [cv.take(2048) for _ in range(2)]
        cx.junk = cv.take(2048)
        fr = [0]
        br = [0]

        def fpool():
            i = fr[0]
            fr[0] = (i + 1) % nf
            return i

        def bpool():
            i = br[0]
            br[0] = (i + 1) % nb
            return i
        cx.fpool = fpool
        cx.bpool = bpool
        return cx

    cv = Carve()
    o_W1a = cv.take(8 * 1536 * 2)
    c1 = make_ctx(cv, 4, 3)
    o_vst = [cv.take(768 * 2) for _ in range(2)]
    o_xs = [cv.take(4096) for _ in range(4)]
    xstate["pool"] = list(xpool) + [((lambda n, o_=o_: rf32(o_, 0, n=n)), ("xs", k_)) for k_, o_ in enumerate(o_xs)]
    p1a_end = cv.off
    cv2 = Carve()
    VBN = 69 * 192
    o_vb1 = cv2.take(VBN * 2)
    o_pt = [cv2.take(2048) for _ in range(3)]
    o_oc = cv2.take(8192)
    o_rz = cv2.take(8192)
    o_kc4 = cv2.take(20 * 128 * 2)
    o_qc4 = cv2.take(16 * 128 * 2)
    o_kc16 = cv2.take(32 * 128 * 2)
    o_qc16 = cv2.take(16 * 128 * 2)
    assert cv2.off <= p1a_end
    p2_low_end = cv2.off
    cv2.off = p1a_end
    o_vb0 = cv2.take(VBN * 2)
    o_mask = cv2.take(2048)
    o_vb = [o_vb0, o_vb1]
    o_mf = o_xs[3]
    mask_ap = rb16(o_mask, 0, n=1024)

    def late_consts():
        fa, ka = xs_alloc()
        fb, kb = xs_alloc()
        DMA("sp", fa(512), wsT_d[:, :], [], [ka])
        DMA("sp", fb(512), ctril_d[:, :], [], [kb])
        TT("dve", fa(512), fa(512), fb(512), ALU.mult, [ka, kb], [ka])
        ACT(wsTb[:, :], fa(512), AF.Copy, [ka], ["wsT"])
        fc, kc_ = xs_alloc()
        DMA("sp", fc(1024), cmask_d[:, :], [], [kc_])
        ACT(mask_ap, fc(1024), AF.Copy, [kc_], ["mask"])

    ngc = lambda kc: cs(C_NG + kc)
    def load_w1a_k():
        for kc in range(8):
            load_weight_piece("sp", "dve", w_in, kc * 128, AK, 512, rb16(o_W1a, kc * 1536 + 512, n=512), ngc(kc),
                              ("W1a", kc, 0), rd=["c_ng"])

    def load_w1a_v():
        for kc in range(8):
            load_weight_piece("sp", "dve", w_in, kc * 128, AV, 512, rb16(o_W1a, kc * 1536 + 1024, n=512), ngc(kc),
                              ("W1a", kc, 2), rd=["c_ng"])

    W1A_ALL = [("W1a", kc, j) for kc in range(8) for j in range(2)]
    W1A_KV = [("W1a", kc, 0) for kc in range(8)]
    W1A_V = [("W1a", kc, 2) for kc in range(8)]

    def w1a_ap(c0):
        return lambda kc: rb16(o_W1a, kc * 1536 + c0, n=128)

    for i in range(2):
        MSET("dve", rb16(o_vst[i], 0, n=768), 1.0, [], [("vst", i)])
        onec = rb16(o_vst[i], 64, dims=[[192, 4], [1, 64]])
        TS("dve", onec, onec, cs(C_HM), None, ALU.mult, None, [("vst", i), "c_hm"], [("vst", i)])
    vst_ctr = [0]

    def v_tile(hTo, tcol, hkey, scr_row0, halo):
        b = next_bank()
        MM([(bank_ap(b, 0, 512), rb16(hTo, kc * 512 + tcol, n=128), rb16(o_W1a, kc * 1536 + 1024, n=512), kc == 0, kc == 7, False)
            for kc in range(8)], W1A_V + [hkey], [("bank", b)])
        vi = vst_ctr[0] % 2
        vst_ctr[0] += 1
        outap = rb16(o_vst[vi], 0, dims=[[192, 4], [128, 2], [1, 64]])
        inap = bank_ap(b, 0, dims=[[128, 4], [64, 2], [1, 64]])
        if halo:
            ACT(outap, inap, AF.Copy, [("bank", b), "c_hm"], [("vst", vi)], scale=cs(C_HM))
        else:
            ACT(outap, inap, AF.Copy, [("bank", b)], [("vst", vi)])
        DMA("act", scr[scr_row0:scr_row0 + 128, :], rb16(o_vst[vi], 0, n=768), [("vst", vi)], [("scr", scr_row0 // 128)])

    def slab_1a_x(si, tiles=(0, 1, 2, 3)):
        halo = si < 4
        src = xh if halo else xo
        row_base = (si % 4) * SL
        hb_i = si % 2
        hTo = c1.hT[hb_i]
        for t in tiles:
            x_tile(c1, src, row_base + t * 128, rb16(hTo, t * 128, dims=[[512, 8], [1, 128]]), ("hT", hb_i, t),
                   prio=(1e6 if si == 0 else 0.0))

    def slab_1a(si):
        halo = si < 4
        hb_i = si % 2
        hTo = c1.hT[hb_i]
        hkeys = [("hT", hb_i, t) for t in range(4)]
        if si > 0:
            slab_1a_x(si)
        if not halo:
            so_ = si - 4
            DMA("sp", hscr[so_ * 128:(so_ + 1) * 128, :], rb16(hTo, 0, n=4096), hkeys, [("hscr", so_)])
        jobs = [("K", c) for c in range(4)]
        if not halo:
            jobs += [("Q", c) for c in range(4)]

        def do_proj(job):
            kind, c = job
            if kind == "K":
                return proj_f2(w1a_ap(512 + c * 128), W1A_KV, hTo, 512, hkeys)
            return proj_f2(w1a_ap(c * 128), W1A_ALL, hTo, 512, hkeys)

        def do_norm(job, b):
            kind, c = job
            if kind == "K":
                unit_norm(c1, b, 512, AP(KT, 0, 128, c * 4096 + si * SL, [[1, 512]]), None, ["KTz"], [("KT", c, si)])
            else:
                so = si - 4
                unit_norm(c1, b, 512, AP(QT, 0, 128, c * NT + so * SL, [[1, 512]]), cs(C_GQ), ["c_gq"],
                          [("QT", 2 * c, so), ("QT", 2 * c + 1, so)])
        bcur = do_proj(jobs[0])
        vt = 0
        for u in range(len(jobs)):
            held.add(bcur)
            bnext = do_proj(jobs[u + 1]) if u + 1 < len(jobs) else None
            if bnext is not None:
                held.add(bnext)
            if vt < 4 and (u % max(1, len(jobs) // 4) == 0):
                v_tile(hTo, vt * 128, ("hT", hb_i, vt), si * SL + vt * 128, halo)
                vt += 1
            do_norm(jobs[u], bcur)
            held.discard(bcur)
            bcur = bnext
        while vt < 4:
            v_tile(hTo, vt * 128, ("hT", hb_i, vt), si * SL + vt * 128, halo)
            vt += 1
        held.clear()

    wpieces = []
    for kc in range(8):
        wpieces.append((w_in, kc * 128, 0, 768, AP(W1b, 0, 128, kc * 1792, [[1, 768]]), ngc(kc), ("W1b", kc, 0), ["c_ng"]))
        wpieces.append((w_in, kc * 128, AGT, 1024, AP(W1b, 0, 128, kc * 1792 + 768, [[1, 1024]]), ngc(kc), ("W1b", kc, 1), ["c_ng"]))
    for kc in range(8):
        wpieces.append((w_out, kc * 128, 0, 1024, AP(Wout, 0, 128, kc * 1024, [[1, 1024]]), 0.5, ("Wout", kc), []))
    slab_1a_x(0)
    load_w1a_k()
    load_w1a_v()
    w1b_next = [0]

    def pump_w1b(n):
        for _ in range(n):
            j = w1b_next[0]
            if j >= 0:
                return
            w1b_next[0] += 1
            dram, row0, col0, ncols, dst, scal, key, rd = wpieces[j]
            xf, xk = xs_alloc()
            DMA("sp", xf(ncols), dram[row0:row0 + 128, col0:col0 + ncols], [], [xk])
            P.op("dve", lambda e, dst=dst, src_=xf(ncols), scal=scal: e.tensor_scalar(out=dst, in0=src_, scalar1=scal, scalar2=None, op0=ALU.mult),
                 [xk] + list(rd), [key], dur=120 + 0.6 * ncols, prio=2e5)

    for si in range(4):
        slab_1a(si)
        pump_w1b(2)
        if si == 3:
            late_consts()
        if si == 2:
            for kc in range(8):
                load_weight_piece("sp", "dve", w_in, kc * 128, AQ, 512, rb16(o_W1a, kc * 1536, n=512), ngc(kc),
                                  ("W1a", kc, 1), rd=["c_ng"])
    for i in range(2):
        MSET("dve", rb16(o_vst[i], 64, dims=[[192, 4], [1, 64]]), 1.0, [("vst", i)], [("vst", i)])
    for si in range(4, 8):
        slab_1a(si)
        pump_w1b(2)

    wstate = dict(next_dma=0, next_cv=0)

    SCR_ALL = [("scr", i) for i in range(32)]

    def load_vblocks(p, vbi):
        vo = o_vb[vbi]
        key = ("vb", vbi)

        def sap(off_rows, dims):
            return bass.AP(scr, off_rows * 768 + p * 192, [list(d) for d in dims])
        for (n0, cnt_) in ((15, 1), (16, 4), (20, 4), (24, 4), (28, 4)):
            DMA("sp", rb16(vo, (n0 - 15) * 192, dims=[[192, cnt_], [1, 192]]),
                sap(n0 * 128, [[768, 128], [128 * 768, cnt_], [1, 192]]), [("scr", n_) for n_ in range(n0, n0 + cnt_)],
                [("vb", vbi, "a", n0)])
        for n in range(3, 8):
            DMA("sp", rb16(vo, (17 + (n - 3) * 4) * 192, dims=[[192, 4], [1, 192]]),
                sap(4 * 128 * n, [[4 * 768, 128], [768, 4], [1, 192]]), [("scr", n_) for n_ in range(4 * n, 4 * n + 4)],
                [("vb", vbi, "b", n)])
        for n in range(2):
            for r0 in (0, 8):
                DMA("sp", rb16(vo, (37 + n * 16 + r0) * 192, dims=[[192, 8], [1, 192]]),
                    sap(16 * 128 * n + r0, [[16 * 768, 128], [768, 8], [1, 192]]), [("scr", n_) for n_ in range(16 * n, 16 * n + 16)],
                    [("vb", vbi, "c", n, r0)])

    def vkey(vbi, d, r, n):
        if d == 1:
            return ("vb", vbi, "a", 15 if n == 15 else 16 + 4 * ((n - 16) // 4))
        if d == 4:
            return ("vb", vbi, "b", n)
        return ("vb", vbi, "c", n, 8 * (r // 8))

    P.barrier()
    xstate["pool"] = list(xpool)
    stg_aps = [xbuf[0], xbuf[1]]

    def stg(i, ncols):
        i = i % 2
        return xbuf[i][:, 0:ncols], ("x", i)

    def pump_weights(flush=False):
        while True:
            did = False
            if wstate["next_cv"] < wstate["next_dma"] and (flush or wstate["next_dma"] - wstate["next_cv"] >= 2 or wstate["next_dma"] == len(wpieces)):
                j = wstate["next_cv"]
                dram, row0, col0, ncols, dst, scal, key, rd = wpieces[j]
                sap_, skey = stg(j, ncols)
                TS("dve", dst, sap_, scal, None, ALU.mult, None, [skey] + list(rd), [key])
                wstate["next_cv"] += 1
                did = True
            if wstate["next_dma"] < len(wpieces) and wstate["next_dma"] - wstate["next_cv"] < 2:
                j = wstate["next_dma"]
                dram, row0, col0, ncols, dst, scal, key, rd = wpieces[j]
                sap_, skey = stg(j, ncols)
                DMA("sp", sap_, dram[row0:row0 + 128, col0:col0 + ncols], [], [skey])
                wstate["next_dma"] += 1
                did = True
            if not flush or not did:
                break
            if wstate["next_cv"] == len(wpieces):
                break
    def units_for(d):
        nb = 32 // d
        return [(r, nq) for r in range(d) for nq in range(nb // 2, nb)]

    def vidx(d, r, n):
        if d == 1:
            return n - 15
        if d == 4:
            return 17 + (n - 3) * 4 + r
        return 37 + n * 16 + r

    pt_rr = [0]
    s_rr = [0]

    def make_contig(p):
        kk = [("KT", p, s) for s in range(8)]
        qq = [("QT", 2 * p + e_, s) for e_ in range(2) for s in range(4)]
        COPY("dve", rb16(o_kc4, 0, n=2560), AP(KT, 0, 128, p * 4096 + 1536, [[1, 4], [512, 5], [4, 128]]), kk, ["kc4"])
        COPY("dve", rb16(o_qc4, 0, n=2048), AP(QT, 0, 128, p * NT, [[1, 4], [512, 4], [4, 128]]), qq, ["qc4"])
        ACT(rb16(o_kc16, 0, n=4096), AP(KT, 0, 128, p * 4096, [[2048, 2], [1, 16], [16, 128]]), AF.Copy, kk, ["kc16"])
        COPY("dve", rb16(o_qc16, 0, n=2048), AP(QT, 0, 128, p * NT, [[1, 16], [16, 128]]), qq, ["qc16"])

    def head_attention(h, vbi):
        p = h // 2
        hr = 64 * (h % 2)
        zr = 64 - hr
        vo = o_vb[vbi]
        vcol = 0 if h % 2 == 0 else 64
        batches = []
        for d in (1, 4, 16):
            us = units_for(d)
            for i in range(0, len(us), 4):
                batches.append((d, us[i:i + 4]))
        kkeys = [("KT", p, s) for s in range(8)]
        qkeys = [("QT", h, s) for s in range(4)]
        started = set()

        OWN = [0, 2, 4, 7]
        PRV = [6, 1, 3, 5]

        def emit_qk(bi):
            d, us = batches[bi]
            sb_ = s_rr[0] % 2
            s_rr[0] += 1
            scol = sb_ * 1024

            def kblk(r, nk):
                if d == 4:
                    return rb16(o_kc4, (r * 5 + nk - 3) * 128, npart=64, p0=hr, n=128)
                if d == 16:
                    return rb16(o_kc16, (nk * 16 + r) * 128, npart=64, p0=hr, n=128)
                return AP(KT, hr, 64, p * 4096 + r + d * 128 * nk, [[d, 128]])

            def qblk(r, nq, n):
                if d == 4:
                    return rb16(o_qc4, (r * 4 + nq - 4) * 128, npart=64, p0=hr, n=n)
                if d == 16:
                    return rb16(o_qc16, r * 128, npart=64, p0=hr, n=n)
                return AP(QT, hr, 64, p * NT + r + d * 128 * nq - 2048, [[d, n]])
            ckeys = {1: [], 4: ["kc4", "qc4"], 16: ["kc16", "qc16"]}[d]

            def sslot(s_, n):
                return bass.AP(psB, scol + s_ * 128, [[2048, 128], [1, n]])
            lst = []
            if d < 16:
                r = us[0][0]
                nqs = [u[1] for u in us]
                for j in range(3):
                    lst.append((sslot(OWN[j], 256), kblk(r, nqs[j]), qblk(r, nqs[j], 256), True, True, False))
                lst.append((sslot(PRV[0], 128), kblk(r, nqs[0] - 1), qblk(r, nqs[0], 128), True, True, False))
                lst.append((sslot(OWN[3], 128), kblk(r, nqs[3]), qblk(r, nqs[3], 128), True, True, False))
            else:
                for j, (r, nq) in enumerate(us):
                    lst.append((sslot(PRV[j], 128), kblk(r, nq - 1), qblk(r, nq, 128), True, True, False))
                    lst.append((sslot(OWN[j], 128), kblk(r, nq), qblk(r, nq, 128), True, True, False))
            MM(lst, (kkeys + qkeys) if d == 1 else ckeys, [("bank", 4 + 2 * sb_), ("bank", 5 + 2 * sb_)])
            return sb_

        def emit_rest(bi, sb_):
            d, us = batches[bi]
            pi = pt_rr[0] % 3
            pt_rr[0] += 1
            scol = sb_ * 1024
            pt_ap = rb16(o_pt[pi], 0, n=1024)
            ACT(pt_ap, bass.AP(psB, scol, [[2048, 128], [1, 1024]]), AF.Exp,
                [("bank", 4 + 2 * sb_), ("bank", 5 + 2 * sb_)], [("pt", pi)])
            TT("dve", pt_ap, pt_ap, mask_ap, ALU.mult, [("pt", pi), "mask"], [("pt", pi)])
            lst = []
            vks = set()

            def vblk(r, nk):
                vks.add(vkey(vbi, d, r, nk))
                return rb16(vo, vidx(d, r, nk) * 192 + vcol, n=128)

            def ocol(r, nq, n):
                return bass.AP(psA, r + d * 128 * nq - 2048, [[2048, 128], [d, n]])

            def pslot(s_, n):
                return rb16(o_pt[pi], s_ * 128, n=n)

            def st_for(r, nq):
                g = (r + d * 128 * nq - 2048) // 512
                s_ = g not in started
                started.add(g)
                return s_
            if d == 1:
                r = us[0][0]
                nqs = [u[1] for u in us]
                for j in range(3):
                    lst.append((ocol(r, nqs[j], 256), vblk(r, nqs[j]), pslot(OWN[j], 256), st_for(r, nqs[j]), True, True))
                lst.append((ocol(r, nqs[0], 128), vblk(r, nqs[0] - 1), pslot(PRV[0], 128), False, True, True))
                lst.append((ocol(r, nqs[3], 128), vblk(r, nqs[3]), pslot(OWN[3], 128), False, True, True))
            elif d == 4:
                for j, (r, nq) in enumerate(us):
                    lst.append((ocol(r, nq, 128), vblk(r, nq - 1), pslot(PRV[j], 128), st_for(r, nq), True, True))
                    lst.append((ocol(r, nq, 128), vblk(r, nq), pslot(OWN[j], 128), False, True, True))
            else:
                for j, (r, nq) in enumerate(us):
                    for (s_, nk) in ((PRV[j], nq - 1), (OWN[j], nq)):
                        vap = vblk(r, nk)
                        for g in range(4):
                            lst.append((bass.AP(psA, 512 * g + r, [[2048, 128], [16, 32]]), vap,
                                        rb16(o_pt[pi], s_ * 128 + 32 * g, n=32), False, True, True))
            if d == 1:
                okeys_ = [("O", (us[0][1] - 16) // 4)]
            else:
                okeys_ = [("O", g_) for g_ in range(4)]
            MM(lst, [("pt", pi)] + sorted(vks, key=str), okeys_)

        sbs = {0: emit_qk(0)}
        for bi in range(len(batches)):
            if bi + 1 < len(batches):
                sbs[bi + 1] = emit_qk(bi + 1)
            emit_rest(bi, sbs[bi])
            if (bi + 12 * (h % 4)) % 3 == 2:
                pump_weights()
        lnz = rf32(o_rz, 0, npart=64, p0=zr, n=2048)
        ocn = rf32(o_oc, 0, npart=64, p0=hr, n=2048)
        for hf in range(2):
            c0_ = hf * 1024
            ok_ = [("O", 2 * hf), ("O", 2 * hf + 1)]
            ACT(rf32(o_rz, c0_, npart=64, p0=zr, n=1024), bass.AP(psA, zr * 2048 + c0_, [[2048, 64], [1, 1024]]), AF.Ln,
                ok_, [("rz", zr, hf)])
            COPY("dve", rf32(o_oc, c0_, npart=64, p0=hr, n=1024), bass.AP(psA, hr * 2048 + c0_, [[2048, 64], [1, 1024]]),
                 ok_, [("oc", hr, hf)])
        RZ_Z = [("rz", zr, 0), ("rz", zr, 1)]
        RZ_H = [("rz", hr, 0), ("rz", hr, 1)]
        OC_H = [("oc", hr, 0), ("oc", hr, 1)]
        if h < 7:
            ACT(lnz, lnz, AF.Exp, RZ_Z, RZ_Z, scale=-1.0)
            rzs = rf32(o_rz, 0, npart=64, p0=hr, n=2048)
            COPY("dve", rzs, lnz, RZ_Z, RZ_H)
            TT("dve", AP(QT, hr, 64, p * NT, [[1, 2048]]), ocn, rzs, ALU.mult, RZ_H + OC_H, [("QT", h, c_) for c_ in range(4)])
        else:
            for hf in range(2):
                c0_ = hf * 1024
                lz = rf32(o_rz, c0_, npart=64, p0=zr, n=1024)
                rs = rf32(o_rz, c0_, npart=64, p0=hr, n=1024)
                ACT(lz, lz, AF.Exp, [("rz", zr, hf)], [("rz", zr, hf)], scale=-1.0)
                COPY("dve", rs, lz, [("rz", zr, hf)], [("rz", hr, hf)])
                TT("dve", AP(QT, hr, 64, p * NT + c0_, [[1, 1024]]), rf32(o_oc, c0_, npart=64, p0=hr, n=1024), rs, ALU.mult,
                   [("rz", hr, hf), ("oc", hr, hf)], [("QT", h, 2 * hf), ("QT", h, 2 * hf + 1)])

    load_vblocks(0, 0)
    for p in range(4):
        if p + 1 < 4:
            load_vblocks(p + 1, (p + 1) % 2)
        make_contig(p)
        head_attention(2 * p, p % 2)
        head_attention(2 * p + 1, p % 2)
    cvu = Carve()
    cvu.off = p1a_end
    c2 = Ctx()
    hT0_ = cvu.take(8 * 512 * 2)
    c2.hb = [cvu.take(2048) for _ in range(2)]
    c2.junk = cvu.take(2048)
    o_wkv = cvu.take(8 * 512 * 2)
    o_bT = cvu.take(256 * 4)
    assert cvu.off <= o_mask
    o_vn = [cvu.take(512) for _ in range(2)]
    cv = Carve()
    c2.hT = [hT0_, cv.take(8 * 512 * 2)]
    c2.f = [cv.take(2048) for _ in range(6)]
    c2.b = [cv.take(1024) for _ in range(4)]
    _fr = [0]
    _br = [0]

    def _fpool():
        i = _fr[0]
        _fr[0] = (i + 1) % 6
        return i

    def _bpool():
        i = _br[0]
        _br[0] = (i + 1) % 4
        return i
    c2.fpool = _fpool
    c2.bpool = _bpool
    o_xr = [cv.take(4096) for _ in range(3)]
    xr_ctr = [0]
    o_xs = [cv.take(4096)]
    o_qm = [cv.take(2 * 512 * 2) for _ in range(2)]
    o_sgm = [cv.take(2 * 512 * 2) for _ in range(2)]
    o_sgg = [cv.take(2 * 512 * 4) for _ in range(2)]
    o_yg = [cv.take(2 * 512 * 2) for _ in range(2)]
    o_ym = [cv.take(2 * 512 * 2) for _ in range(2)]
    o_sqv = [cv.take(1024) for _ in range(2)]
    o_mkt = cv.take(2 * 256 * 2)
    o_mv = cv.take(2 * 2 * 192 * 2)
    assert cv.off <= p2_low_end, (cv.off, p2_low_end)
    cv.off = p2_low_end
    o_hmT = cv.take(8 * 256 * 2)
    assert cv.off <= p1a_end, cv.off
    VB0_KEYS = ([("vb", 0, "a", n0) for n0 in (15, 16, 20, 24, 28)] + [("vb", 0, "b", n) for n in range(3, 8)]
                + [("vb", 0, "c", n, r0) for n in range(2) for r0 in (0, 8)])

    def slab_1b_x(so, extra_r=()):
        hb_i = so % 2
        hTo = c2.hT[hb_i]
        DMA("sp", rb16(hTo, 0, n=4096), hscr[so * 128:(so + 1) * 128, :], [("hscr", so)] + list(extra_r),
            [("hT", hb_i, t) for t in range(4)])

    xstate["pool"] = list(xpool)
    P.op("sp", lambda e: e.nop(), [], VB0_KEYS + ["vb0free"], dur=50)
    slab_1b_x(0, extra_r=["vb0free"])
    P.default_prio = -40000.0
    DMA("sp", rf32(o_bT, 0, n=256), bT_d[:, :], ["vb0free"], ["bT"])
    for kc in range(8):
        load_weight_piece("sp", "dve", w_kv, kc * 128, 0, 512, rb16(o_wkv, kc * 512, n=512), cs(C_MNG + kc),
                          ("Wkv", kc), rd=["c_mng", "vb0free"])
    for t in range(2):
        x_tile(c2, mem_d, t * 128, rb16(o_hmT, t * 128, dims=[[256, 8], [1, 128]]), ("hmT", t), extra_r=["vb0free"], tbank=7)
    P.default_prio = 0.0
    pump_weights(flush=True)

    P.barrier()
    def preamble_1b():
        WKV = [("Wkv", kc) for kc in range(8)]
        HMT = [("hmT", 0), ("hmT", 1)]
        for c in range(2):
            b = proj_f2(lambda kc, c=c: rb16(o_wkv, kc * 512 + c * 128, n=128), WKV, o_hmT, 256, HMT, hstride=256)
            unit_norm(c2, b, 256, rb16(o_mkt, c * 256, n=256), cs(C_GMK), ["c_gmk"], [("mkt", c)])
        MSET("dve", rb16(o_mv, 0, n=768), 1.0, [], ["mv"])
        for t in range(2):
            b = next_bank()
            MM([(bank_ap(b, 0, 256), rb16(o_hmT, kc * 256 + t * 128, n=128), rb16(o_wkv, kc * 512 + 256, n=256), kc == 0, kc == 7, False)
                for kc in range(8)], WKV + HMT, [("bank", b)])
            ACT(rb16(o_mv, t * 384, dims=[[192, 2], [128, 2], [1, 64]]), bank_ap(b, 0, dims=[[128, 2], [64, 2], [1, 64]]), AF.Copy,
                [("bank", b), "mv"], ["mv"])

    WKV = [("Wkv", kc) for kc in range(8)]
    HMT = [("hmT", 0), ("hmT", 1)]
    W1B = [("W1b", kc, j) for kc in range(8) for j in range(2)]
    WOUT = [("Wout", kc) for kc in range(8)]

    def w1b_ap(c0):
        return lambda kc: AP(W1b, 0, 128, kc * 1792 + c0, [[1, 128]])

    def gate_chunk(c0, hTo, hkeys, out_ap, okeys):
        b = proj_f2(w1b_ap(c0), W1B, hTo, 512, hkeys)
        f1 = c2.fpool()
        f1_ap = rf32(c2.f[f1], 0, n=512)
        ACT(f1_ap, bank_ap(b, 0, 512), AF.Tanh, [("bank", b)], [("fp", f1)], scale=0.5)
        STT(out_ap, f1_ap, 1.0, bank_ap(b, 0, 512), ALU.add, ALU.mult, [("bank", b), ("fp", f1)], list(okeys))

    def slab_1b(so):
        hb_i = so % 2
        sl2 = so % 2
        hTo = c2.hT[hb_i]
        hkeys = [("hT", hb_i, t) for t in range(4)]
        if so > 0:
            slab_1b_x(so)
        for c in range(4):
            f2 = c2.fpool()
            f2_ap = rf32(c2.f[f2], 0, n=512)
            gate_chunk(768 + c * 128, hTo, hkeys, f2_ap, [("fp", f2)])
            qa = AP(QT, 0, 128, c * NT + so * SL, [[1, 512]])
            TT("dve", qa, qa, f2_ap, ALU.mult, [("fp", f2), ("QT", 2 * c, so), ("QT", 2 * c + 1, so)],
               [("QT", 2 * c, so), ("QT", 2 * c + 1, so)])
        for c in range(2):
            b = proj_f2(w1b_ap(1280 + c * 128), W1B, hTo, 512, hkeys)
            unit_norm(c2, b, 512, rb16(o_qm[sl2], c * 512, n=512), None, [], [("qm", sl2, c)])
            gate_chunk(1536 + c * 128, hTo, hkeys, rb16(o_sgm[sl2], c * 512, n=512), [("sgm", sl2, c)])
        for hm_ in range(4):
            cm = hm_ // 2
            hr = 64 * (hm_ % 2)
            zr = 64 - hr
            vcol = 0 if hm_ % 2 == 0 else 64
            pms = []
            for j in range(2):
                b = next_bank()
                MM([(bank_ap(b, 0, 512), rb16(o_mkt, cm * 256 + j * 128, npart=64, p0=hr, n=128),
                     rb16(o_qm[sl2], cm * 512, npart=64, p0=hr, n=512), True, True, False)],
                   [("mkt", cm), ("qm", sl2, cm)], [("bank", b)])
                pm = c2.bpool()
                ACT(rb16(c2.b[pm], 0, n=512), bank_ap(b, 0, 512), AF.Exp, [("bank", b)], [("bp", pm)])
                pms.append(pm)
            bo = next_bank()
            MM([(bank_ap(bo, 0, 512), rb16(o_mv, j * 384 + cm * 192 + vcol, n=128), rb16(c2.b[pms[j]], 0, n=512), j == 0, j == 1, False)
                for j in range(2)], ["mv", ("bp", pms[0]), ("bp", pms[1])], [("bank", bo)])
            f1 = c2.fpool()
            f1z = rf32(c2.f[f1], 0, npart=64, p0=zr, n=512)
            f1h = rf32(c2.f[f1], 0, npart=64, p0=hr, n=512)
            ACT(f1z, bank_ap(bo, 0, 512, p0=zr, npart=64), AF.Ln, [("bank", bo)], [("fp", f1)])
            ACT(f1z, f1z, AF.Exp, [("fp", f1)], [("fp", f1)], scale=-1.0)
            COPY("dve", f1h, f1z, [("fp", f1)], [("fp", f1)])
            f2 = c2.fpool()
            f2_ap = rf32(c2.f[f2], 0, npart=64, p0=hr, n=512)
            TT("dve", f2_ap, bank_ap(bo, 0, 512, p0=hr, npart=64), f1h, ALU.mult, [("bank", bo), ("fp", f1)], [("fp", f2)])
            TT("dve", rb16(o_ym[sl2], cm * 512, npart=64, p0=hr, n=512), f2_ap, rb16(o_sgm[sl2], cm * 512, npart=64, p0=hr, n=512),
               ALU.mult, [("fp", f2), ("sgm", sl2, cm)], [("ym", sl2, hm_)])
        spb = [next_bank(), next_bank()]
        held.update(spb)
        for t in range(4):
            b = next_bank()
            MM([(bank_ap(b, 0, 256), rb16(hTo, kc * 512 + t * 128, n=128), AP(W1b, 0, 128, kc * 1792 + 256, [[1, 256]]), kc == 0, kc == 7, False)
                for kc in range(8)], W1B + [("hT", hb_i, t)], [("bank", b)])
            vi = t % 2
            ACT(rf32(o_sqv[vi], 0, n=256), bank_ap(b, 0, 256), AF.Square, [("bank", b)], [("sqv", vi)])
            tt = tile_ctr[0]
            tile_ctr[0] += 1
            g0 = (tt % 8) * 12
            P.op("dve", lambda e, vi=vi, g0=g0: e.tensor_reduce(out=gst[:, g0:g0 + 4], in_=rf32(o_sqv[vi], 0, dims=[[64, 4], [1, 64]]),
                                                                axis=AX.X, op=ALU.add),
                 [("sqv", vi)], [("gst", tt % 8, 0)])
            ACT(gst[:, g0 + 4:g0 + 8], gst[:, g0:g0 + 4], AF.Ln, [("gst", tt % 8, 0)], [("gst", tt % 8, 1)], scale=1.0 / 64, bias=EPS)
            ACT(gst[:, g0 + 8:g0 + 12], gst[:, g0 + 4:g0 + 8], AF.Exp, [("gst", tt % 8, 1)], [("gst", tt % 8, 2)], scale=-0.5)
            TT("dve", rb16(o_vn[vi], 0, dims=[[64, 4], [1, 64]]), bank_ap(b, 0, dims=[[64, 4], [1, 64]]),
               bass.AP(gst, g0 + 8, [[96, 128], [1, 4], [0, 64]]), ALU.mult, [("bank", b), ("gst", tt % 8, 2)], [("vn", vi)])
            MM([(bank_ap(spb[hh // 2], t * 128, 128, p0=64 * (hh % 2), npart=64), rb16(o_vn[vi], hh * 64, n=64),
                 wsTb[:, hh * 128:(hh + 1) * 128], True, True, True) for hh in range(4)],
               [("vn", vi), "wsT"], [("bank", spb[0]), ("bank", spb[1])])
        for c in range(2):
            gate_chunk(512 + c * 128, hTo, hkeys, rf32(o_sgg[sl2], c * 512, n=512), [("sgg", sl2, c)])
        for pp in range(2):
            gb = proj_f2(w1b_ap(pp * 128), W1B, hTo, 512, hkeys)
            fa = c2.fpool()
            STT(rf32(c2.f[fa], 0, dims=[[128, 4], [1, 128]]), bank_ap(spb[pp], 0, dims=[[128, 4], [1, 128]]), cs(C_VG + pp),
                rf32(o_bT, pp * 128, dims=[[0, 4], [1, 128]]), ALU.mult, ALU.add, [("bank", spb[pp]), "bT", "c_vg"], [("fp", fa)])
            fb = c2.fpool()
            TT("dve", rf32(c2.f[fb], 0, n=512), bank_ap(gb, 0, 512), rf32(c2.f[fa], 0, n=512), ALU.mult,
               [("bank", gb), ("fp", fa)], [("fp", fb)])
            TT("dve", rb16(o_yg[sl2], pp * 512, n=512), rf32(c2.f[fb], 0, n=512), rf32(o_sgg[sl2], pp * 512, n=512), ALU.mult,
               [("fp", fb), ("sgg", sl2, pp)], [("yg", sl2, pp)])
        held.clear()
        ykeys_g = [("yg", sl2, 0), ("yg", sl2, 1)]
        ykeys_am = [("QT", h, so) for h in range(8)] + [("ym", sl2, h) for h in range(4)]
        for t in range(4):
            ri = xr_ctr[0] % 3
            xr_ctr[0] += 1
            row0 = so * SL + t * 128
            xr_full = rf32(o_xr[ri], 0, n=1024)
            DMA("sp", xr_full, xo[row0:row0 + 128, :], [], [("xr", ri, 0), ("xr", ri, 1)])
            for half in range(2):
                b = next_bank()
                def lt_of(c):
                    if c < 2:
                        return rb16(o_yg[sl2], c * 512 + t * 128, n=128)
                    if c < 6:
                        return AP(QT, 0, 128, (c - 2) * NT + so * SL + t * 128, [[1, 128]])
                    return rb16(o_ym[sl2], (c - 6) * 512 + t * 128, n=128)
                MM([(bank_ap(b, 0, 512), lt_of(c), AP(Wout, 0, 128, c * 1024 + half * 512, [[1, 512]]), c == 2, False, False)
                    for c in (2, 3, 4, 5, 6, 7)], ykeys_am + WOUT, [("bank", b)])
                MM([(bank_ap(b, 0, 512), lt_of(c), AP(Wout, 0, 128, c * 1024 + half * 512, [[1, 512]]), False, c == 1, False)
                    for c in (0, 1)], ykeys_g + WOUT, [("bank", b)])
                xh_ = rf32(o_xr[ri], half * 512, n=512)
                TT("dve", xh_, bank_ap(b, 0, 512), xh_, ALU.add, [("bank", b), ("xr", ri, half)], [("xr", ri, half)])
            DMA("sp", out_d[row0:row0 + 128, :], xr_full, [("xr", ri, 0), ("xr", ri, 1)], [("out", row0)])

    xstate["pool"] = list(xpool) + [((lambda n, o_=o_: rf32(o_, 0, n=n)), ("xs", k_)) for k_, o_ in enumerate(o_xs)]
    nbanks[0] = 8
    P.default_prio = -40000.0
    preamble_1b()
    P.default_prio = 0.0
    for so in range(4):
        slab_1b(so)
    P.op("sp", lambda e: e.nop(), [("out", so * SL + t * 128) for so in range(4) for t in range(4)], [])
    P.emit()
    return nc


_NC_CACHE = {}


def kernel(x, mem, norm_gain, w_in, gmlp_v_gain, gmlp_w_s, gmlp_b, attn_q_gain, attn_k_gain,
           mem_norm_gain, w_mem_kv, mem_q_gain, mem_k_gain, w_out):
    f = np.float32
    x = np.asarray(x, f)
    mem = np.asarray(mem, f)
    B, S, D = x.shape
    w_in0 = np.ascontiguousarray(np.asarray(w_in, f)[0])
    w_out0 = np.ascontiguousarray(np.asarray(w_out, f)[0])
    w_kv0 = np.ascontiguousarray(np.asarray(w_mem_kv, f)[0])
    ng = np.ascontiguousarray(np.asarray(norm_gain, f)[0].reshape(8, 128).T)
    mng = np.ascontiguousarray(np.asarray(mem_norm_gain, f)[0].reshape(8, 128).T)
    vgn = np.asarray(gmlp_v_gain, f)[0]
    vg = np.ascontiguousarray(vgn.reshape(2, 128).T)
    ws = np.asarray(gmlp_w_s, f)[0]
    wsT = np.ascontiguousarray(ws.transpose(2, 0, 1).reshape(128, 512))
    bb = np.asarray(gmlp_b, f)[0]
    bT = np.ascontiguousarray(np.repeat(bb.reshape(2, 2, 1, 128), 64, axis=2).reshape(2, 128, 128).transpose(1, 0, 2).reshape(128, 256))
    ag = np.ascontiguousarray(np.stack([np.tile(np.asarray(attn_q_gain, f)[0], 2), np.tile(np.asarray(attn_k_gain, f)[0], 2)], 1))
    mg = np.ascontiguousarray(np.stack([np.tile(np.asarray(mem_q_gain, f)[0], 2), np.tile(np.asarray(mem_k_gain, f)[0], 2)], 1))
    ii = np.arange(128)
    ident = (ii[:, None] == ii[None, :]).astype(f)
    blkc = ((ii[:, None] // 64) == (ii[None, :] // 64)).astype(f) / 64.0
    cid = np.ascontiguousarray(np.concatenate([ident, blkc], axis=1))
    own = (ii[None, :] >= ii[:, None]).astype(f)
    prv = (ii[:, None] >= ii[None, :]).astype(f)
    cmask = np.ascontiguousarray(np.concatenate([own, prv, own, prv, own, prv, prv, own], axis=1))
    ctril = np.ascontiguousarray(np.tile(own, (1, 4)))
    if "nc" not in _NC_CACHE:
        _NC_CACHE["nc"] = build_nc()
    nc = _NC_CACHE["nc"]
    in_maps = []
    for c in range(8):
        b, half = c // 2, c % 2
        xo = np.ascontiguousarray(x[b, half * NT:(half + 1) * NT])
        if half == 0:
            xh = np.zeros((NT, D), f)
            hmv = np.zeros((128, 1), f)
        else:
            xh = np.ascontiguousarray(x[b, 0:NT])
            hmv = np.ones((128, 1), f)
        in_maps.append(dict(xo=xo, xh=xh, hm=hmv, mem=np.ascontiguousarray(mem[b]), w_in=w_in0, w_out=w_out0,
                            w_kv=w_kv0, ng=ng, mng=mng, vg=vg, wsT=wsT, bT=bT, ag=ag, mg=mg,
                            cid=cid, cmask=cmask, ctril=ctril))
    res = run_bass_kernel_spmd(nc, in_maps, core_ids=list(range(8)))
    out = np.empty((B, S, D), f)
    for c in range(8):
        b, half = c // 2, c % 2
        out[b, half * NT:(half + 1) * NT] = res.results[c]["out"]
    return out
```

```python
import numpy as np
import concourse.bass as bass
import concourse.mybir as mybir
from concourse.bass_utils import run_bass_kernel_spmd

F32 = mybir.dt.float32
BF16 = mybir.dt.bfloat16
AF = mybir.ActivationFunctionType
ALU = mybir.AluOpType
AX = mybir.AxisListType

NT = 2048
SL = 512
EPS = 1e-6
GU, GV, GG, AQ, AK, AV, AGT, MQ, MG = 0, 256, 512, 768, 1280, 1792, 2304, 2816, 3072


class Prog:
    ENG = ("pe", "act", "dve", "pool", "sp")

    def __init__(self, nc):
        self.nc = nc
        self.ops = []
        self.last_w = {}
        self.readers = {}
        self.ndma_sems = 16
        self.bar_from = 0
        self.default_prio = 0.0

    def op(self, eng, fn, reads=(), writes=(), dma=False, dur=None, lat=None, prio=0.0, tbl=None):
        i = len(self.ops)
        deps = set()
        for k in reads:
            w = self.last_w.get(k)
            if w is not None:
                deps.add(w)
        for k in writes:
            w = self.last_w.get(k)
            if w is not None:
                deps.add(w)
            for r in self.readers.get(k, ()):
                deps.add(r)
        deps.discard(i)
        self.ops.append(dict(eng=eng, fn=fn, deps=deps, dma=dma, dur=dur, lat=lat, fence=False,
                             prio=(prio if prio else self.default_prio), wr=list(writes), tbl=tbl))
        for k in reads:
            self.readers.setdefault(k, []).append(i)
        for k in writes:
            self.last_w[k] = i
            self.readers[k] = []
        return i

    def barrier(self):
        deps = set()
        last = {}
        for i in range(self.bar_from, len(self.ops)):
            o = self.ops[i]
            if o["dma"]:
                deps.add(i)
            else:
                last[o["eng"]] = i
        deps |= set(last.values())
        first = None
        for e in self.ENG:
            i = len(self.ops)
            d = set(deps) if first is None else {first}
            self.ops.append(dict(eng=e, fn=(lambda eng: eng.nop()), deps=d, dma=False, dur=50, lat=None, fence=True, prio=0.0, wr=["FENCE"], tbl=None))
            if first is None:
                first = i
        self.bar_from = len(self.ops)

    def schedule(self):
        ops = self.ops
        DEF = dict(pe=300, act=700, dve=700, pool=1500, sp=150)
        order = []
        seg = []
        segs = []
        for i, o in enumerate(ops):
            if o["fence"]:
                if seg:
                    segs.append(("ops", seg))
                    seg = []
                segs.append(("fence", [i]))
            else:
                seg.append(i)
        if seg:
            segs.append(("ops", seg))
        prev_seg_order = []
        for kind, seg in segs:
            if kind == "fence":
                fi = seg[0]
                last = {}
                for j in prev_seg_order:
                    if not ops[j]["dma"]:
                        last[ops[j]["eng"]] = j
                if ops[fi]["deps"] and len(ops[fi]["deps"]) > 1:
                    ops[fi]["deps"] |= set(last.values())
                order.extend(seg)
                continue
            seg_start = len(order)
            inseg = set(seg)
            ndeps = {}
            users = {}
            for i in seg:
                ds = [d for d in ops[i]["deps"] if d in inseg]
                ndeps[i] = len(ds)
                for d in ds:
                    users.setdefault(d, []).append(i)
            def dur_of(o):
                if o["dma"]:
                    return o["lat"] if o["lat"] is not None else 4000.0
                return o["dur"] if o["dur"] is not None else DEF[o["eng"]]
            blev = {}
            for i in reversed(seg):
                m = 0.0
                for u in users.get(i, ()):
                    if blev[u] > m:
                        m = blev[u]
                blev[i] = m + dur_of(ops[i]) + 800.0 + ops[i]["prio"]
            ready = [i for i in seg if ndeps[i] == 0]
            est = {}
            for i in ready:
                est[i] = 0.0
            fin = {}
            etime = {e: 0.0 for e in self.ENG}
            dma_free = [0.0]
            act_tbl = [6]
            nleft = len(seg)
            while nleft:
                best = None
                bkey = None
                for i in ready:
                    o = ops[i]
                    et = etime[o["eng"]]
                    if o["tbl"] is not None and o["tbl"] != act_tbl[0]:
                        et = et + 1300.0
                    if est[i] <= et:
                        key = (et, 0, -blev[i], i)
                    else:
                        key = (est[i], 1, -blev[i], i)
                    if bkey is None or key < bkey:
                        bkey = key
                        best = i
                i = best
                ready.remove(i)
                o = ops[i]
                st = bkey[0]
                dur = o["dur"] if o["dur"] is not None else DEF[o["eng"]]
                if o["dma"]:
                    occ = 350.0 if o["eng"] == "act" else (600.0 if o["eng"] == "pool" else 150.0)
                    etime[o["eng"]] = st + occ
                    xfer = (o["lat"] if o["lat"] is not None else 4000.0) - 2500.0
                    x0 = max(st + occ, dma_free[0])
                    dma_free[0] = x0 + xfer
                    fin[i] = dma_free[0] + 2000.0
                else:
                    etime[o["eng"]] = st + dur
                    fin[i] = st + dur
                if o["tbl"] is not None:
                    act_tbl[0] = o["tbl"]
                order.append(i)
                nleft -= 1
                for u in users.get(i, ()):
                    ndeps[u] -= 1
                    lat = 150.0 if (ops[u]["eng"] == o["eng"] and not o["dma"]) else 800.0
                    t = fin[i] + lat
                    if est.get(u, 0.0) < t:
                        est[u] = t
                    if ndeps[u] == 0:
                        ready.append(u)
            prev_seg_order = order[seg_start:]
        assert len(order) == len(ops)
        return order

    def _check_deadlock(self, ops, per_eng, dsem, sem):
        val = {}
        pos = {e: 0 for e in self.ENG}
        total = sum(len(v) for v in per_eng.values())
        done = 0
        progress = True
        while progress and done < total:
            progress = False
            for e in self.ENG:
                while pos[e] < len(per_eng[e]):
                    i = per_eng[e][pos[e]]
                    o = ops[i]
                    ok = True
                    for d in o["deps"]:
                        po = ops[d]
                        if po["sig"] is None:
                            continue
                        if po["eng"] == "pe" and e == "pe" and not po["dma"]:
                            continue
                        s, v = po["sig"]
                        if val.get(id(s), 0) < v:
                            ok = False
                            break
                    if ok and o["dma"]:
                        j = o["dma_idx"]
                        if j >= self.ndma_sems:
                            s = dsem[e][j % self.ndma_sems]
                            if val.get(id(s), 0) < 16 * (j // self.ndma_sems):
                                ok = False
                    if not ok:
                        break
                    if o["sig"] is not None:
                        s, v = o["sig"]
                        val[id(s)] = val.get(id(s), 0) + (16 if o["dma"] else 1)
                        assert val[id(s)] == v or o["dma"], (i, val[id(s)], v)
                    pos[e] += 1
                    done += 1
                    progress = True
        if done < total:
            stuck = {e: (per_eng[e][pos[e]] if pos[e] < len(per_eng[e]) else None) for e in self.ENG}
            raise RuntimeError("semaphore deadlock; stuck ops: %r" % (stuck,))

    def emit(self, reorder=True):
        nc = self.nc
        ops = self.ops
        order = self.schedule() if reorder else list(range(len(ops)))
        needed = set()
        for i, o in enumerate(ops):
            for d in o["deps"]:
                po = ops[d]
                if po["eng"] == "pe" and o["eng"] == "pe" and not po["dma"]:
                    continue
                needed.add(d)
        sem = {e: nc.alloc_semaphore("s_" + e) for e in ("pe", "act", "dve", "pool", "sp")}
        dsem = {}
        for e in ("sp", "pool", "act"):
            dsem[e] = [nc.alloc_semaphore("d_%s%d" % (e, j)) for j in range(self.ndma_sems)]
        cnt = {e: 0 for e in sem}
        dcnt = {e: 0 for e in dsem}
        for i in order:
            o = ops[i]
            if o["dma"]:
                e = o["eng"]
                j = dcnt[e]
                dcnt[e] += 1
                o["sig"] = (dsem[e][j % self.ndma_sems], 16 * (j // self.ndma_sems + 1))
                o["dma_idx"] = j
            elif i in needed:
                cnt[o["eng"]] += 1
                o["sig"] = (sem[o["eng"]], cnt[o["eng"]])
            else:
                o["sig"] = None
        per_eng = {e: [] for e in self.ENG}
        for i in order:
            per_eng[ops[i]["eng"]].append(i)

        self._check_deadlock(ops, per_eng, dsem, sem)
        import os as _os
        if _os.environ.get("KDEBUG"):
            for e in self.ENG:
                print("ENGINE", e, len(per_eng[e]))
                for i in per_eng[e][:int(_os.environ.get("KDEBUG"))]:
                    print("   ", i, ops[i]["wr"][:3], "dma" if ops[i]["dma"] else "")

        def run_engine(ename, eng):
            waited = {}
            for i in per_eng[ename]:
                o = ops[i]
                waits = {}
                for d in o["deps"]:
                    po = ops[d]
                    if po["sig"] is None:
                        continue
                    if po["eng"] == "pe" and ename == "pe" and not po["dma"]:
                        continue
                    s, v = po["sig"]
                    key = id(s)
                    if waits.get(key, (None, -1))[1] < v:
                        waits[key] = (s, v)
                if o["dma"]:
                    j = o["dma_idx"]
                    if j >= self.ndma_sems:
                        s = dsem[ename][j % self.ndma_sems]
                        v = 16 * (j // self.ndma_sems)
                        key = id(s)
                        if waits.get(key, (None, -1))[1] < v:
                            waits[key] = (s, v)
                for key, (s, v) in waits.items():
                    if waited.get(key, -1) >= v:
                        continue
                    eng.wait_ge(s, v)
                    waited[key] = v
                ins = o["fn"](eng)
                if o["sig"] is not None:
                    s, v = o["sig"]
                    ins.then_inc(s, 16 if o["dma"] else 1)

        with nc.Block() as block:
            @block.tensor
            def _(e):
                run_engine("pe", e)

            @block.scalar
            def _(e):
                run_engine("act", e)

            @block.vector
            def _(e):
                run_engine("dve", e)

            @block.gpsimd
            def _(e):
                run_engine("pool", e)

            @block.sync
            def _(e):
                run_engine("sp", e)


def build_nc():
    nc = bass.Bass("TRN2", target_bir_lowering=False)
    P = Prog(nc)

    def din(name, shape, dt=F32):
        return nc.dram_tensor(name, list(shape), dt, kind="ExternalInput")

    xo = din("xo", [NT, 1024])
    xh = din("xh", [NT, 1024])
    hm_d = din("hm", [128, 1])
    mem_d = din("mem", [256, 1024])
    w_in = din("w_in", [1024, 3328])
    w_out = din("w_out", [1024, 1024])
    w_kv = din("w_kv", [1024, 512])
    ng_d = din("ng", [128, 8])
    mng_d = din("mng", [128, 8])
    vg_d = din("vg", [128, 2])
    wsT_d = din("wsT", [128, 512])
    bT_d = din("bT", [128, 256])
    ag_d = din("ag", [128, 2])
    mg_d = din("mg", [128, 2])
    cid_d = din("cid", [128, 256])
    cmask_d = din("cmask", [128, 1024])
    ctril_d = din("ctril", [128, 512])
    out_d = nc.dram_tensor("out", [NT, 1024], F32, kind="ExternalOutput")
    scr = nc.dram_tensor("scr", [4096, 768], BF16, kind="Internal")
    hscr = nc.dram_tensor("hscr", [4 * 128, 4096], BF16, kind="Internal")

    def sb(name, cols, dt):
        return nc.alloc_sbuf_tensor(name, [128, cols], dt)

    def AP(t, p0, npart, col, dims):
        Fr = t.shape[1]
        return bass.AP(t, p0 * Fr + col, [[Fr, npart]] + [list(d) for d in dims])

    def fsz(ap):
        n = 1
        for s_ in ap.shape[1:]:
            n *= s_
        return n

    def ACT(out, in_, func, reads, writes, **kw):
        tbl = 0 if func == AF.Tanh else (6 if func == AF.Ln else None)
        P.op("act", lambda e: e.activation(out=out, in_=in_, func=func, **kw), reads, writes, dur=230 + 0.84 * fsz(in_), tbl=tbl)

    def TS(eng, out, in0, s1, s2, op0, op1, reads, writes):
        if op1 is None:
            P.op(eng, lambda e: e.tensor_scalar(out=out, in0=in0, scalar1=s1, scalar2=None, op0=op0), reads, writes, dur=120 + 0.6 * fsz(in0))
        else:
            P.op(eng, lambda e: e.tensor_scalar(out=out, in0=in0, scalar1=s1, scalar2=s2, op0=op0, op1=op1), reads, writes, dur=120 + 0.6 * fsz(in0))

    def TT(eng, out, in0, in1, op, reads, writes):
        P.op(eng, lambda e: e.tensor_tensor(out=out, in0=in0, in1=in1, op=op), reads, writes, dur=120 + 1.05 * fsz(in0))

    def STT(out, in0, scalar, in1, op0, op1, reads, writes):
        P.op("dve", lambda e: e.scalar_tensor_tensor(out=out, in0=in0, scalar=scalar, in1=in1, op0=op0, op1=op1), reads, writes, dur=120 + 1.05 * fsz(in0))

    def COPY(eng, out, in_, reads, writes):
        P.op(eng, lambda e: e.tensor_copy(out=out, in_=in_), reads, writes, dur=120 + 0.6 * fsz(in_))

    def MSET(eng, ap, val, reads, writes):
        P.op(eng, lambda e: e.memset(ap, val), reads, writes, dur=300 + 0.9 * fsz(ap))

    def MM(lst, reads, writes):
        lst = list(lst)

        def f(e):
            ins = None
            for (o, l, r, st, sp, sk) in lst:
                if sk:
                    ins = e.matmul(o, lhsT=l, rhs=r, start=st, stop=sp, skip_group_check=True)
                else:
                    ins = e.matmul(o, lhsT=l, rhs=r, start=st, stop=sp)
            return ins
        d_ = 0.0
        for (o, l, r, st, sp, sk) in lst:
            n_ = fsz(r)
            d_ += max(55 + 0.43 * n_, 150.0) if n_ > 32 else 75.0
        P.op("pe", f, reads, writes, dur=d_)

    def TR(lst, reads, writes):
        lst = list(lst)

        def f(e):
            ins = None
            for (o, i_) in lst:
                ins = e.transpose(out=o, in_=i_, identity=ident[:, :])
            return ins
        P.op("pe", f, list(reads) + ["ident"], writes, dur=130.0 * len(lst))

    def DMA(q, out, in_, reads, writes, prio=0.0):
        nbytes = fsz(out) * out.shape[0] * (2 if out.dtype == BF16 else 4)
        P.op(q, lambda e: e.dma_start(out=out, in_=in_), reads, writes, dma=True, lat=2500.0 + nbytes / 150.0, prio=prio)

    def ASEL(out, pattern, cm, reads, writes):
        P.op("pool", lambda e: e.affine_select(out=out, in_=out, pattern=pattern, compare_op=ALU.is_ge, fill=0.0,
                                               base=0, channel_multiplier=cm), reads, writes)

    KT = sb("KT", 4 * 4096, BF16)
    QT = sb("QT", 4 * NT, BF16)
    Wout = sb("Wout", 8 * 1024, BF16)
    W1b = sb("W1b", 8 * 1792, BF16)
    xbuf = [sb("xbuf%d" % i, 1024, F32) for i in range(2)]
    ident = sb("ident", 128, BF16)
    blk = sb("blk", 128, BF16)
    cst = sb("cst", 64, F32)
    stats = sb("stats", 3 * 80, F32)
    gst = sb("gst", 96, F32)
    C_HM, C_NG, C_MNG, C_VG, C_AG, C_MG, C_GQ, C_GMK = 0, 1, 9, 17, 19, 21, 23, 24

    def cs(c, n=1):
        return cst[:, c:c + n]

    RB = 104 * 1024
    R = nc.alloc_sbuf_tensor("R", [128, RB // 2], BF16)
    Rf = R.bitcast(F32)

    class Carve:
        def __init__(self):
            self.off = 0

        def take(self, nbytes):
            o = self.off
            self.off += (nbytes + 63) // 64 * 64
            assert self.off <= RB, self.off
            return o

    def rb16(off_bytes, col=0, npart=128, p0=0, dims=None, n=None):
        base = off_bytes // 2 + col
        if dims is None:
            dims = [[1, n]]
        return bass.AP(R, p0 * (RB // 2) + base, [[RB // 2, npart]] + [list(d) for d in dims])

    def rf32(off_bytes, col=0, npart=128, p0=0, dims=None, n=None):
        base = off_bytes // 4 + col
        if dims is None:
            dims = [[1, n]]
        return bass.AP(Rf, p0 * (RB // 4) + base, [[RB // 4, npart]] + [list(d) for d in dims])

    psA = nc.alloc_psum_tensor("psA", [128, 2048], F32)
    psB = nc.alloc_psum_tensor("psB", [128, 2048], F32)
    psBb = psB.bitcast(BF16)

    def bank_ap(i, col=0, n=512, p0=0, npart=128, dims=None):
        t = psA if i < 4 else psB
        c = (i % 4) * 512 + col
        if dims is None:
            dims = [[1, n]]
        return bass.AP(t, p0 * 2048 + c, [[2048, npart]] + [list(d) for d in dims])

    bank_rr = [0]
    nbanks = [8]
    held = set()

    def next_bank():
        while True:
            b = bank_rr[0]
            bank_rr[0] = (b + 1) % nbanks[0]
            if b not in held:
                return b

    psAb = psA.bitcast(BF16)

    def pT_ap(bk, col, n):
        tb = psAb if bk < 4 else psBb
        return bass.AP(tb, (bk % 4) * 1024 + col, [[4096, 128], [1, n]])

    def pT_all(bk):
        tb = psAb if bk < 4 else psBb
        return bass.AP(tb, (bk % 4) * 1024, [[4096, 128], [128, 8], [1, 128]])

    DMA("sp", cs(C_HM), hm_d[:, :], [], ["c_hm"])
    DMA("sp", cs(C_NG, 8), ng_d[:, :], [], ["c_ng"])
    DMA("sp", cs(C_MNG, 8), mng_d[:, :], [], ["c_mng"])
    DMA("sp", cs(C_VG, 2), vg_d[:, :], [], ["c_vg"])
    DMA("sp", cs(C_AG, 2), ag_d[:, :], [], ["c_ag"])
    DMA("sp", cs(C_MG, 2), mg_d[:, :], [], ["c_mg"])
    TS("dve", cs(C_GQ), cs(C_AG), cs(C_AG + 1), None, ALU.mult, None, ["c_ag"], ["c_gq"])
    TS("dve", cs(C_GQ), cs(C_GQ), 0.125, None, ALU.mult, None, ["c_gq"], ["c_gq"])
    TS("dve", cs(C_GMK), cs(C_MG), cs(C_MG + 1), None, ALU.mult, None, ["c_mg"], ["c_gmk"])
    TS("dve", cs(C_GMK), cs(C_GMK), 0.125, None, ALU.mult, None, ["c_gmk"], ["c_gmk"])
    idf = xbuf[1]
    DMA("sp", idf[:, 0:256], cid_d[:, :], [], [("x", 1)])
    ACT(ident[:, :], idf[:, 0:128], AF.Copy, [("x", 1)], ["ident"])
    ACT(blk[:, :], idf[:, 128:256], AF.Copy, [("x", 1)], ["blk"])
    KTf = KT.bitcast(F32)
    P.op("act", lambda e: e.memzero(KTf[:, :]), [], ["KTz"], dur=7000.0)
    wsTb = sb("wsTb", 512, BF16)

    wq = [0]
    tile_ctr = [0]

    xpool = [(lambda n, t_=xbuf[0]: t_[:, 0:n], ("x", 0)), (lambda n, t_=xbuf[1]: t_[:, 0:n], ("x", 1))]
    xstate = dict(pool=list(xpool))

    def xs_alloc():
        pl = xstate["pool"]
        i = wq[0] % len(pl)
        wq[0] += 1
        return pl[i]

    def load_weight_piece(q, ceng, dram, row0, col0, ncols, dst_ap, scal, key, rd=(), prio=0.0):
        xf, xk = xs_alloc()
        DMA(q, xf(ncols), dram[row0:row0 + 128, col0:col0 + ncols], [], [xk], prio=prio)
        TS(ceng, dst_ap, xf(ncols), scal, None, ALU.mult, None, [xk] + list(rd), [key])

    class Ctx:
        pass

    def x_tile(cx, src, row0, hT_out_ap, ktag, prio=0.0, extra_r=(), tbank=None):
        t = tile_ctr[0]
        tile_ctr[0] += 1
        xf, xk = xs_alloc()
        xfull = xf(1024)
        sc = (t % 80) * 3
        hbi = t % 2
        hb_ap = rb16(cx.hb[hbi], 0, n=1024)
        DMA("sp", xfull, src[row0:row0 + 128, :], list(extra_r), [xk], prio=prio)
        ACT(rb16(cx.junk, 0, n=1024), xfull, AF.Square, [xk], [("st", t, 0), "junk"], accum_out=stats[:, sc:sc + 1])
        ACT(stats[:, sc + 1:sc + 2], stats[:, sc:sc + 1], AF.Ln, [("st", t, 0)], [("st", t, 1)], scale=1.0 / 1024, bias=EPS)
        ACT(stats[:, sc + 2:sc + 3], stats[:, sc + 1:sc + 2], AF.Exp, [("st", t, 1)], [("st", t, 2)], scale=-0.5)
        TS("dve", hb_ap, xfull, stats[:, sc + 2:sc + 3], None, ALU.mult, None, [xk, ("st", t, 2)], [("hb", hbi)])
        bk = next_bank() if tbank is None else tbank
        TR([(pT_ap(bk, k * 128, 128), rb16(cx.hb[hbi], k * 128, n=128)) for k in range(8)], [("hb", hbi)], [("bank", bk)])
        COPY("dve", hT_out_ap, pT_all(bk), [("bank", bk)], [ktag])

    def proj_f2(wap_fn, wkeys, hTo, ncols, hkeys, hstride=512):
        b = next_bank()
        MM([(bank_ap(b, 0, ncols), wap_fn(kc), rb16(hTo, kc * hstride, n=ncols), kc == 0, kc == 7, False) for kc in range(8)],
           list(wkeys) + list(hkeys), [("bank", b)])
        return b

    def unit_norm(cx, b, ncols, out_ap, gain_ap, gkeys, okeys):
        sq = cx.bpool()
        sq_ap = rb16(cx.b[sq], 0, n=ncols)
        ACT(sq_ap, bank_ap(b, 0, ncols), AF.Square, [("bank", b)], [("bp", sq)])
        b2 = next_bank()
        MM([(bank_ap(b2, 0, ncols), blk[:, :], sq_ap, True, True, False)], [("bp", sq), "blk"], [("bank", b2)])
        f1 = cx.fpool()
        f1_ap = rf32(cx.f[f1], 0, n=ncols)
        ACT(f1_ap, bank_ap(b2, 0, ncols), AF.Ln, [("bank", b2)], [("fp", f1)], bias=EPS)
        f2 = cx.fpool()
        f2_ap = rf32(cx.f[f2], 0, n=ncols)
        ACT(f2_ap, f1_ap, AF.Exp, [("fp", f1)], [("fp", f2)], scale=-0.5)
        sc = gain_ap if gain_ap is not None else 1.0
        STT(out_ap, bank_ap(b, 0, ncols), sc, f2_ap, ALU.mult, ALU.mult, [("bank", b), ("fp", f2)] + list(gkeys), list(okeys))

    def make_ctx(cv, nf, nb):
        cx = Ctx()
        cx.hT = [cv.take(8 * 512 * 2) for _ in range(2)]
        cx.f = [cv.take(2048) for _ in range(nf)]
        cx.b = [cv.take(1024) for _ in range(nb)]
        cx.hb = [cv.take(2048) for _ in range(2)]
        cx.junk = cv.take(2048)
        fr = [0]
        br = [0]

        def fpool():
            i = fr[0]
            fr[0] = (i + 1) % nf
            return i

        def bpool():
            i = br[0]
            br[0] = (i + 1) % nb
            return i
        cx.fpool = fpool
        cx.bpool = bpool
        return cx

    cv = Carve()
    o_W1a = cv.take(8 * 1536 * 2)
    c1 = make_ctx(cv, 4, 3)
    o_vst = [cv.take(768 * 2) for _ in range(2)]
    o_xs = [cv.take(4096) for _ in range(4)]
    xstate["pool"] = list(xpool) + [((lambda n, o_=o_: rf32(o_, 0, n=n)), ("xs", k_)) for k_, o_ in enumerate(o_xs)]
    p1a_end = cv.off
    cv2 = Carve()
    VBN = 69 * 192
    o_vb1 = cv2.take(VBN * 2)
    o_pt = [cv2.take(2048) for _ in range(3)]
    o_oc = cv2.take(8192)
    o_rz = cv2.take(8192)
    o_kc4 = cv2.take(20 * 128 * 2)
    o_qc4 = cv2.take(16 * 128 * 2)
    o_kc16 = cv2.take(32 * 128 * 2)
    o_qc16 = cv2.take(16 * 128 * 2)
    assert cv2.off <= p1a_end
    p2_low_end = cv2.off
    cv2.off = p1a_end
    o_vb0 = cv2.take(VBN * 2)
    o_mask = cv2.take(2048)
    o_vb = [o_vb0, o_vb1]
    o_mf = o_xs[3]
    mask_ap = rb16(o_mask, 0, n=1024)

    def late_consts():
        fa, ka = xs_alloc()
        fb, kb = xs_alloc()
        DMA("sp", fa(512), wsT_d[:, :], [], [ka])
        DMA("sp", fb(512), ctril_d[:, :], [], [kb])
        TT("dve", fa(512), fa(512), fb(512), ALU.mult, [ka, kb], [ka])
        ACT(wsTb[:, :], fa(512), AF.Copy, [ka], ["wsT"])
        fc, kc_ = xs_alloc()
        DMA("sp", fc(1024), cmask_d[:, :], [], [kc_])
        ACT(mask_ap, fc(1024), AF.Copy, [kc_], ["mask"])

    ngc = lambda kc: cs(C_NG + kc)
    def load_w1a_k():
        for kc in range(8):
            load_weight_piece("sp", "dve", w_in, kc * 128, AK, 512, rb16(o_W1a, kc * 1536 + 512, n=512), ngc(kc),
                              ("W1a", kc, 0), rd=["c_ng"])

    def load_w1a_v():
        for kc in range(8):
            load_weight_piece("sp", "dve", w_in, kc * 128, AV, 512, rb16(o_W1a, kc * 1536 + 1024, n=512), ngc(kc),
                              ("W1a", kc, 2), rd=["c_ng"])

    W1A_ALL = [("W1a", kc, j) for kc in range(8) for j in range(2)]
    W1A_KV = [("W1a", kc, 0) for kc in range(8)]
    W1A_V = [("W1a", kc, 2) for kc in range(8)]

    def w1a_ap(c0):
        return lambda kc: rb16(o_W1a, kc * 1536 + c0, n=128)

    for i in range(2):
        MSET("dve", rb16(o_vst[i], 0, n=768), 1.0, [], [("vst", i)])
        onec = rb16(o_vst[i], 64, dims=[[192, 4], [1, 64]])
        TS("dve", onec, onec, cs(C_HM), None, ALU.mult, None, [("vst", i), "c_hm"], [("vst", i)])
    vst_ctr = [0]

    def v_tile(hTo, tcol, hkey, scr_row0, halo):
        b = next_bank()
        MM([(bank_ap(b, 0, 512), rb16(hTo, kc * 512 + tcol, n=128), rb16(o_W1a, kc * 1536 + 1024, n=512), kc == 0, kc == 7, False)
            for kc in range(8)], W1A_V + [hkey], [("bank", b)])
        vi = vst_ctr[0] % 2
        vst_ctr[0] += 1
        outap = rb16(o_vst[vi], 0, dims=[[192, 4], [128, 2], [1, 64]])
        inap = bank_ap(b, 0, dims=[[128, 4], [64, 2], [1, 64]])
        if halo:
            ACT(outap, inap, AF.Copy, [("bank", b), "c_hm"], [("vst", vi)], scale=cs(C_HM))
        else:
            ACT(outap, inap, AF.Copy, [("bank", b)], [("vst", vi)])
        DMA("act", scr[scr_row0:scr_row0 + 128, :], rb16(o_vst[vi], 0, n=768), [("vst", vi)], [("scr", scr_row0 // 128)])

    def slab_1a_x(si, tiles=(0, 1, 2, 3)):
        halo = si < 4
        src = xh if halo else xo
        row_base = (si % 4) * SL
        hb_i = si % 2
        hTo = c1.hT[hb_i]
        for t in tiles:
            x_tile(c1, src, row_base + t * 128, rb16(hTo, t * 128, dims=[[512, 8], [1, 128]]), ("hT", hb_i, t),
                   prio=(1e6 if si == 0 else 0.0))

    def slab_1a(si):
        halo = si < 4
        hb_i = si % 2
        hTo = c1.hT[hb_i]
        hkeys = [("hT", hb_i, t) for t in range(4)]
        if si > 0:
            slab_1a_x(si)
        if not halo:
            so_ = si - 4
            DMA("sp", hscr[so_ * 128:(so_ + 1) * 128, :], rb16(hTo, 0, n=4096), hkeys, [("hscr", so_)])
        jobs = [("K", c) for c in range(4)]
        if not halo:
            jobs += [("Q", c) for c in range(4)]

        def do_proj(job):
            kind, c = job
            if kind == "K":
                return proj_f2(w1a_ap(512 + c * 128), W1A_KV, hTo, 512, hkeys)
            return proj_f2(w1a_ap(c * 128), W1A_ALL, hTo, 512, hkeys)

        def do_norm(job, b):
            kind, c = job
            if kind == "K":
                unit_norm(c1, b, 512, AP(KT, 0, 128, c * 4096 + si * SL, [[1, 512]]), None, ["KTz"], [("KT", c, si)])
            else:
                so = si - 4
                unit_norm(c1, b, 512, AP(QT, 0, 128, c * NT + so * SL, [[1, 512]]), cs(C_GQ), ["c_gq"],
                          [("QT", 2 * c, so), ("QT", 2 * c + 1, so)])
        bcur = do_proj(jobs[0])
        vt = 0
        for u in range(len(jobs)):
            held.add(bcur)
            bnext = do_proj(jobs[u + 1]) if u + 1 < len(jobs) else None
            if bnext is not None:
                held.add(bnext)
            if vt < 4 and (u % max(1, len(jobs) // 4) == 0):
                v_tile(hTo, vt * 128, ("hT", hb_i, vt), si * SL + vt * 128, halo)
                vt += 1
            do_norm(jobs[u], bcur)
            held.discard(bcur)
            bcur = bnext
        while vt < 4:
            v_tile(hTo, vt * 128, ("hT", hb_i, vt), si * SL + vt * 128, halo)
            vt += 1
        held.clear()

    wpieces = []
    for kc in range(8):
        wpieces.append((w_in, kc * 128, 0, 768, AP(W1b, 0, 128, kc * 1792, [[1, 768]]), ngc(kc), ("W1b", kc, 0), ["c_ng"]))
        wpieces.append((w_in, kc * 128, AGT, 1024, AP(W1b, 0, 128, kc * 1792 + 768, [[1, 1024]]), ngc(kc), ("W1b", kc, 1), ["c_ng"]))
    for kc in range(8):
        wpieces.append((w_out, kc * 128, 0, 1024, AP(Wout, 0, 128, kc * 1024, [[1, 1024]]), 0.5, ("Wout", kc), []))
    slab_1a_x(0)
    load_w1a_k()
    load_w1a_v()
    w1b_next = [0]

    def pump_w1b(n):
        for _ in range(n):
            j = w1b_next[0]
            if j >= 0:
                return
            w1b_next[0] += 1
            dram, row0, col0, ncols, dst, scal, key, rd = wpieces[j]
            xf, xk = xs_alloc()
            DMA("sp", xf(ncols), dram[row0:row0 + 128, col0:col0 + ncols], [], [xk])
            P.op("dve", lambda e, dst=dst, src_=xf(ncols), scal=scal: e.tensor_scalar(out=dst, in0=src_, scalar1=scal, scalar2=None, op0=ALU.mult),
                 [xk] + list(rd), [key], dur=120 + 0.6 * ncols, prio=2e5)

    for si in range(4):
        slab_1a(si)
        pump_w1b(2)
        if si == 3:
            late_consts()
        if si == 2:
            for kc in range(8):
                load_weight_piece("sp", "dve", w_in, kc * 128, AQ, 512, rb16(o_W1a, kc * 1536, n=512), ngc(kc),
                                  ("W1a", kc, 1), rd=["c_ng"])
    for i in range(2):
        MSET("dve", rb16(o_vst[i], 64, dims=[[192, 4], [1, 64]]), 1.0, [("vst", i)], [("vst", i)])
    for si in range(4, 8):
        slab_1a(si)
        pump_w1b(2)

    wstate = dict(next_dma=0, next_cv=0)

    SCR_ALL = [("scr", i) for i in range(32)]

    def load_vblocks(p, vbi):
        vo = o_vb[vbi]
        key = ("vb", vbi)

        def sap(off_rows, dims):
            return bass.AP(scr, off_rows * 768 + p * 192, [list(d) for d in dims])
        for (n0, cnt_) in ((15, 1), (16, 4), (20, 4), (24, 4), (28, 4)):
            DMA("sp", rb16(vo, (n0 - 15) * 192, dims=[[192, cnt_], [1, 192]]),
                sap(n0 * 128, [[768, 128], [128 * 768, cnt_], [1, 192]]), [("scr", n_) for n_ in range(n0, n0 + cnt_)],
                [("vb", vbi, "a", n0)])
        for n in range(3, 8):
            DMA("sp", rb16(vo, (17 + (n - 3) * 4) * 192, dims=[[192, 4], [1, 192]]),
                sap(4 * 128 * n, [[4 * 768, 128], [768, 4], [1, 192]]), [("scr", n_) for n_ in range(4 * n, 4 * n + 4)],
                [("vb", vbi, "b", n)])
        for n in range(2):
            for r0 in (0, 8):
                DMA("sp", rb16(vo, (37 + n * 16 + r0) * 192, dims=[[192, 8], [1, 192]]),
                    sap(16 * 128 * n + r0, [[16 * 768, 128], [768, 8], [1, 192]]), [("scr", n_) for n_ in range(16 * n, 16 * n + 16)],
                    [("vb", vbi, "c", n, r0)])

    def vkey(vbi, d, r, n):
        if d == 1:
            return ("vb", vbi, "a", 15 if n == 15 else 16 + 4 * ((n - 16) // 4))
        if d == 4:
            return ("vb", vbi, "b", n)
        return ("vb", vbi, "c", n, 8 * (r // 8))

    P.barrier()
    xstate["pool"] = list(xpool)
    stg_aps = [xbuf[0], xbuf[1]]

    def stg(i, ncols):
        i = i % 2
        return xbuf[i][:, 0:ncols], ("x", i)

    def pump_weights(flush=False):
        while True:
            did = False
            if wstate["next_cv"] < wstate["next_dma"] and (flush or wstate["next_dma"] - wstate["next_cv"] >= 2 or wstate["next_dma"] == len(wpieces)):
                j = wstate["next_cv"]
                dram, row0, col0, ncols, dst, scal, key, rd = wpieces[j]
                sap_, skey = stg(j, ncols)
                TS("dve", dst, sap_, scal, None, ALU.mult, None, [skey] + list(rd), [key])
                wstate["next_cv"] += 1
                did = True
            if wstate["next_dma"] < len(wpieces) and wstate["next_dma"] - wstate["next_cv"] < 2:
                j = wstate["next_dma"]
                dram, row0, col0, ncols, dst, scal, key, rd = wpieces[j]
                sap_, skey = stg(j, ncols)
                DMA("sp", sap_, dram[row0:row0 + 128, col0:col0 + ncols], [], [skey])
                wstate["next_dma"] += 1
                did = True
            if not flush or not did:
                break
            if wstate["next_cv"] == len(wpieces):
                break
    def units_for(d):
        nb = 32 // d
        return [(r, nq) for r in range(d) for nq in range(nb // 2, nb)]

    def vidx(d, r, n):
        if d == 1:
            return n - 15
        if d == 4:
            return 17 + (n - 3) * 4 + r
        return 37 + n * 16 + r

    pt_rr = [0]
    s_rr = [0]

    def make_contig(p):
        kk = [("KT", p, s) for s in range(8)]
        qq = [("QT", 2 * p + e_, s) for e_ in range(2) for s in range(4)]
        COPY("dve", rb16(o_kc4, 0, n=2560), AP(KT, 0, 128, p * 4096 + 1536, [[1, 4], [512, 5], [4, 128]]), kk, ["kc4"])
        COPY("dve", rb16(o_qc4, 0, n=2048), AP(QT, 0, 128, p * NT, [[1, 4], [512, 4], [4, 128]]), qq, ["qc4"])
        ACT(rb16(o_kc16, 0, n=4096), AP(KT, 0, 128, p * 4096, [[2048, 2], [1, 16], [16, 128]]), AF.Copy, kk, ["kc16"])
        COPY("dve", rb16(o_qc16, 0, n=2048), AP(QT, 0, 128, p * NT, [[1, 16], [16, 128]]), qq, ["qc16"])

    def head_attention(h, vbi):
        p = h // 2
        hr = 64 * (h % 2)
        zr = 64 - hr
        vo = o_vb[vbi]
        vcol = 0 if h % 2 == 0 else 64
        batches = []
        for d in (1, 4, 16):
            us = units_for(d)
            for i in range(0, len(us), 4):
                batches.append((d, us[i:i + 4]))
        kkeys = [("KT", p, s) for s in range(8)]
        qkeys = [("QT", h, s) for s in range(4)]
        started = set()

        OWN = [0, 2, 4, 7]
        PRV = [6, 1, 3, 5]

        def emit_qk(bi):
            d, us = batches[bi]
            sb_ = s_rr[0] % 2
            s_rr[0] += 1
            scol = sb_ * 1024

            def kblk(r, nk):
                if d == 4:
                    return rb16(o_kc4, (r * 5 + nk - 3) * 128, npart=64, p0=hr, n=128)
                if d == 16:
                    return rb16(o_kc16, (nk * 16 + r) * 128, npart=64, p0=hr, n=128)
                return AP(KT, hr, 64, p * 4096 + r + d * 128 * nk, [[d, 128]])

            def qblk(r, nq, n):
                if d == 4:
                    return rb16(o_qc4, (r * 4 + nq - 4) * 128, npart=64, p0=hr, n=n)
                if d == 16:
                    return rb16(o_qc16, r * 128, npart=64, p0=hr, n=n)
                return AP(QT, hr, 64, p * NT + r + d * 128 * nq - 2048, [[d, n]])
            ckeys = {1: [], 4: ["kc4", "qc4"], 16: ["kc16", "qc16"]}[d]

            def sslot(s_, n):
                return bass.AP(psB, scol + s_ * 128, [[2048, 128], [1, n]])
            lst = []
            if d < 16:
                r = us[0][0]
                nqs = [u[1] for u in us]
                for j in range(3):
                    lst.append((sslot(OWN[j], 256), kblk(r, nqs[j]), qblk(r, nqs[j], 256), True, True, False))
                lst.append((sslot(PRV[0], 128), kblk(r, nqs[0] - 1), qblk(r, nqs[0], 128), True, True, False))
                lst.append((sslot(OWN[3], 128), kblk(r, nqs[3]), qblk(r, nqs[3], 128), True, True, False))
            else:
                for j, (r, nq) in enumerate(us):
                    lst.append((sslot(PRV[j], 128), kblk(r, nq - 1), qblk(r, nq, 128), True, True, False))
                    lst.append((sslot(OWN[j], 128), kblk(r, nq), qblk(r, nq, 128), True, True, False))
            MM(lst, (kkeys + qkeys) if d == 1 else ckeys, [("bank", 4 + 2 * sb_), ("bank", 5 + 2 * sb_)])
            return sb_

        def emit_rest(bi, sb_):
            d, us = batches[bi]
            pi = pt_rr[0] % 3
            pt_rr[0] += 1
            scol = sb_ * 1024
            pt_ap = rb16(o_pt[pi], 0, n=1024)
            ACT(pt_ap, bass.AP(psB, scol, [[2048, 128], [1, 1024]]), AF.Exp,
                [("bank", 4 + 2 * sb_), ("bank", 5 + 2 * sb_)], [("pt", pi)])
            TT("dve", pt_ap, pt_ap, mask_ap, ALU.mult, [("pt", pi), "mask"], [("pt", pi)])
            lst = []
            vks = set()

            def vblk(r, nk):
                vks.add(vkey(vbi, d, r, nk))
                return rb16(vo, vidx(d, r, nk) * 192 + vcol, n=128)

            def ocol(r, nq, n):
                return bass.AP(psA, r + d * 128 * nq - 2048, [[2048, 128], [d, n]])

            def pslot(s_, n):
                return rb16(o_pt[pi], s_ * 128, n=n)

            def st_for(r, nq):
                g = (r + d * 128 * nq - 2048) // 512
                s_ = g not in started
                started.add(g)
                return s_
            if d == 1:
                r = us[0][0]
                nqs = [u[1] for u in us]
                for j in range(3):
                    lst.append((ocol(r, nqs[j], 256), vblk(r, nqs[j]), pslot(OWN[j], 256), st_for(r, nqs[j]), True, True))
                lst.append((ocol(r, nqs[0], 128), vblk(r, nqs[0] - 1), pslot(PRV[0], 128), False, True, True))
                lst.append((ocol(r, nqs[3], 128), vblk(r, nqs[3]), pslot(OWN[3], 128), False, True, True))
            elif d == 4:
                for j, (r, nq) in enumerate(us):
                    lst.append((ocol(r, nq, 128), vblk(r, nq - 1), pslot(PRV[j], 128), st_for(r, nq), True, True))
                    lst.append((ocol(r, nq, 128), vblk(r, nq), pslot(OWN[j], 128), False, True, True))
            else:
                for j, (r, nq) in enumerate(us):
                    for (s_, nk) in ((PRV[j], nq - 1), (OWN[j], nq)):
                        vap = vblk(r, nk)
                        for g in range(4):
                            lst.append((bass.AP(psA, 512 * g + r, [[2048, 128], [16, 32]]), vap,
                                        rb16(o_pt[pi], s_ * 128 + 32 * g, n=32), False, True, True))
            if d == 1:
                okeys_ = [("O", (us[0][1] - 16) // 4)]
            else:
                okeys_ = [("O", g_) for g_ in range(4)]
            MM(lst, [("pt", pi)] + sorted(vks, key=str), okeys_)

        sbs = {0: emit_qk(0)}
        for bi in range(len(batches)):
            if bi + 1 < len(batches):
                sbs[bi + 1] = emit_qk(bi + 1)
            emit_rest(bi, sbs[bi])
            if (bi + 12 * (h % 4)) % 3 == 2:
                pump_weights()
        lnz = rf32(o_rz, 0, npart=64, p0=zr, n=2048)
        ocn = rf32(o_oc, 0, npart=64, p0=hr, n=2048)
        for hf in range(2):
            c0_ = hf * 1024
            ok_ = [("O", 2 * hf), ("O", 2 * hf + 1)]
            ACT(rf32(o_rz, c0_, npart=64, p0=zr, n=1024), bass.AP(psA, zr * 2048 + c0_, [[2048, 64], [1, 1024]]), AF.Ln,
                ok_, [("rz", zr, hf)])
            COPY("dve", rf32(o_oc, c0_, npart=64, p0=hr, n=1024), bass.AP(psA, hr * 2048 + c0_, [[2048, 64], [1, 1024]]),
                 ok_, [("oc", hr, hf)])
        RZ_Z = [("rz", zr, 0), ("rz", zr, 1)]
        RZ_H = [("rz", hr, 0), ("rz", hr, 1)]
        OC_H = [("oc", hr, 0), ("oc", hr, 1)]
        if h < 7:
            ACT(lnz, lnz, AF.Exp, RZ_Z, RZ_Z, scale=-1.0)
            rzs = rf32(o_rz, 0, npart=64, p0=hr, n=2048)
            COPY("dve", rzs, lnz, RZ_Z, RZ_H)
            TT("dve", AP(QT, hr, 64, p * NT, [[1, 2048]]), ocn, rzs, ALU.mult, RZ_H + OC_H, [("QT", h, c_) for c_ in range(4)])
        else:
            for hf in range(2):
                c0_ = hf * 1024
                lz = rf32(o_rz, c0_, npart=64, p0=zr, n=1024)
                rs = rf32(o_rz, c0_, npart=64, p0=hr, n=1024)
                ACT(lz, lz, AF.Exp, [("rz", zr, hf)], [("rz", zr, hf)], scale=-1.0)
                COPY("dve", rs, lz, [("rz", zr, hf)], [("rz", hr, hf)])
                TT("dve", AP(QT, hr, 64, p * NT + c0_, [[1, 1024]]), rf32(o_oc, c0_, npart=64, p0=hr, n=1024), rs, ALU.mult,
                   [("rz", hr, hf), ("oc", hr, hf)], [("QT", h, 2 * hf), ("QT", h, 2 * hf + 1)])

    load_vblocks(0, 0)
    for p in range(4):
        if p + 1 < 4:
            load_vblocks(p + 1, (p + 1) % 2)
        make_contig(p)
        head_attention(2 * p, p % 2)
        head_attention(2 * p + 1, p % 2)
    cvu = Carve()
    cvu.off = p1a_end
    c2 = Ctx()
    hT0_ = cvu.take(8 * 512 * 2)
    c2.hb = [cvu.take(2048) for _ in range(2)]
    c2.junk = cvu.take(2048)
    o_wkv = cvu.take(8 * 512 * 2)
    o_bT = cvu.take(256 * 4)
    assert cvu.off <= o_mask
    o_vn = [cvu.take(512) for _ in range(2)]
    cv = Carve()
    c2.hT = [hT0_, cv.take(8 * 512 * 2)]
    c2.f = [cv.take(2048) for _ in range(6)]
    c2.b = [cv.take(1024) for _ in range(4)]
    _fr = [0]
    _br = [0]

    def _fpool():
        i = _fr[0]
        _fr[0] = (i + 1) % 6
        return i

    def _bpool():
        i = _br[0]
        _br[0] = (i + 1) % 4
        return i
    c2.fpool = _fpool
    c2.bpool = _bpool
    o_xr = [cv.take(4096) for _ in range(3)]
    xr_ctr = [0]
    o_xs = [cv.take(4096)]
    o_qm = [cv.take(2 * 512 * 2) for _ in range(2)]
    o_sgm = [cv.take(2 * 512 * 2) for _ in range(2)]
    o_sgg = [cv.take(2 * 512 * 4) for _ in range(2)]
    o_yg = [cv.take(2 * 512 * 2) for _ in range(2)]
    o_ym = [cv.take(2 * 512 * 2) for _ in range(2)]
    o_sqv = [cv.take(1024) for _ in range(2)]
    o_mkt = cv.take(2 * 256 * 2)
    o_mv = cv.take(2 * 2 * 192 * 2)
    assert cv.off <= p2_low_end, (cv.off, p2_low_end)
    cv.off = p2_low_end
    o_hmT = cv.take(8 * 256 * 2)
    assert cv.off <= p1a_end, cv.off
    VB0_KEYS = ([("vb", 0, "a", n0) for n0 in (15, 16, 20, 24, 28)] + [("vb", 0, "b", n) for n in range(3, 8)]
                + [("vb", 0, "c", n, r0) for n in range(2) for r0 in (0, 8)])

    def slab_1b_x(so, extra_r=()):
        hb_i = so % 2
        hTo = c2.hT[hb_i]
        DMA("sp", rb16(hTo, 0, n=4096), hscr[so * 128:(so + 1) * 128, :], [("hscr", so)] + list(extra_r),
            [("hT", hb_i, t) for t in range(4)])

    xstate["pool"] = list(xpool)
    P.op("sp", lambda e: e.nop(), [], VB0_KEYS + ["vb0free"], dur=50)
    slab_1b_x(0, extra_r=["vb0free"])
    P.default_prio = -40000.0
    DMA("sp", rf32(o_bT, 0, n=256), bT_d[:, :], ["vb0free"], ["bT"])
    for kc in range(8):
        load_weight_piece("sp", "dve", w_kv, kc * 128, 0, 512, rb16(o_wkv, kc * 512, n=512), cs(C_MNG + kc),
                          ("Wkv", kc), rd=["c_mng", "vb0free"])
    for t in range(2):
        x_tile(c2, mem_d, t * 128, rb16(o_hmT, t * 128, dims=[[256, 8], [1, 128]]), ("hmT", t), extra_r=["vb0free"], tbank=7)
    P.default_prio = 0.0
    pump_weights(flush=True)

    P.barrier()
    def preamble_1b():
        WKV = [("Wkv", kc) for kc in range(8)]
        HMT = [("hmT", 0), ("hmT", 1)]
        for c in range(2):
            b = proj_f2(lambda kc, c=c: rb16(o_wkv, kc * 512 + c * 128, n=128), WKV, o_hmT, 256, HMT, hstride=256)
            unit_norm(c2, b, 256, rb16(o_mkt, c * 256, n=256), cs(C_GMK), ["c_gmk"], [("mkt", c)])
        MSET("dve", rb16(o_mv, 0, n=768), 1.0, [], ["mv"])
        for t in range(2):
            b = next_bank()
            MM([(bank_ap(b, 0, 256), rb16(o_hmT, kc * 256 + t * 128, n=128), rb16(o_wkv, kc * 512 + 256, n=256), kc == 0, kc == 7, False)
                for kc in range(8)], WKV + HMT, [("bank", b)])
            ACT(rb16(o_mv, t * 384, dims=[[192, 2], [128, 2], [1, 64]]), bank_ap(b, 0, dims=[[128, 2], [64, 2], [1, 64]]), AF.Copy,
                [("bank", b), "mv"], ["mv"])

    WKV = [("Wkv", kc) for kc in range(8)]
    HMT = [("hmT", 0), ("hmT", 1)]
    W1B = [("W1b", kc, j) for kc in range(8) for j in range(2)]
    WOUT = [("Wout", kc) for kc in range(8)]

    def w1b_ap(c0):
        return lambda kc: AP(W1b, 0, 128, kc * 1792 + c0, [[1, 128]])

    def gate_chunk(c0, hTo, hkeys, out_ap, okeys):
        b = proj_f2(w1b_ap(c0), W1B, hTo, 512, hkeys)
        f1 = c2.fpool()
        f1_ap = rf32(c2.f[f1], 0, n=512)
        ACT(f1_ap, bank_ap(b, 0, 512), AF.Tanh, [("bank", b)], [("fp", f1)], scale=0.5)
        STT(out_ap, f1_ap, 1.0, bank_ap(b, 0, 512), ALU.add, ALU.mult, [("bank", b), ("fp", f1)], list(okeys))

    def slab_1b(so):
        hb_i = so % 2
        sl2 = so % 2
        hTo = c2.hT[hb_i]
        hkeys = [("hT", hb_i, t) for t in range(4)]
        if so > 0:
            slab_1b_x(so)
        for c in range(4):
            f2 = c2.fpool()
            f2_ap = rf32(c2.f[f2], 0, n=512)
            gate_chunk(768 + c * 128, hTo, hkeys, f2_ap, [("fp", f2)])
            qa = AP(QT, 0, 128, c * NT + so * SL, [[1, 512]])
            TT("dve", qa, qa, f2_ap, ALU.mult, [("fp", f2), ("QT", 2 * c, so), ("QT", 2 * c + 1, so)],
               [("QT", 2 * c, so), ("QT", 2 * c + 1, so)])
        for c in range(2):
            b = proj_f2(w1b_ap(1280 + c * 128), W1B, hTo, 512, hkeys)
            unit_norm(c2, b, 512, rb16(o_qm[sl2], c * 512, n=512), None, [], [("qm", sl2, c)])
            gate_chunk(1536 + c * 128, hTo, hkeys, rb16(o_sgm[sl2], c * 512, n=512), [("sgm", sl2, c)])
        for hm_ in range(4):
            cm = hm_ // 2
            hr = 64 * (hm_ % 2)
            zr = 64 - hr
            vcol = 0 if hm_ % 2 == 0 else 64
            pms = []
            for j in range(2):
                b = next_bank()
                MM([(bank_ap(b, 0, 512), rb16(o_mkt, cm * 256 + j * 128, npart=64, p0=hr, n=128),
                     rb16(o_qm[sl2], cm * 512, npart=64, p0=hr, n=512), True, True, False)],
                   [("mkt", cm), ("qm", sl2, cm)], [("bank", b)])
                pm = c2.bpool()
                ACT(rb16(c2.b[pm], 0, n=512), bank_ap(b, 0, 512), AF.Exp, [("bank", b)], [("bp", pm)])
                pms.append(pm)
            bo = next_bank()
            MM([(bank_ap(bo, 0, 512), rb16(o_mv, j * 384 + cm * 192 + vcol, n=128), rb16(c2.b[pms[j]], 0, n=512), j == 0, j == 1, False)
                for j in range(2)], ["mv", ("bp", pms[0]), ("bp", pms[1])], [("bank", bo)])
            f1 = c2.fpool()
            f1z = rf32(c2.f[f1], 0, npart=64, p0=zr, n=512)
            f1h = rf32(c2.f[f1], 0, npart=64, p0=hr, n=512)
            ACT(f1z, bank_ap(bo, 0, 512, p0=zr, npart=64), AF.Ln, [("bank", bo)], [("fp", f1)])
            ACT(f1z, f1z, AF.Exp, [("fp", f1)], [("fp", f1)], scale=-1.0)
            COPY("dve", f1h, f1z, [("fp", f1)], [("fp", f1)])
            f2 = c2.fpool()
            f2_ap = rf32(c2.f[f2], 0, npart=64, p0=hr, n=512)
            TT("dve", f2_ap, bank_ap(bo, 0, 512, p0=hr, npart=64), f1h, ALU.mult, [("bank", bo), ("fp", f1)], [("fp", f2)])
            TT("dve", rb16(o_ym[sl2], cm * 512, npart=64, p0=hr, n=512), f2_ap, rb16(o_sgm[sl2], cm * 512, npart=64, p0=hr, n=512),
               ALU.mult, [("fp", f2), ("sgm", sl2, cm)], [("ym", sl2, hm_)])
        spb = [next_bank(), next_bank()]
        held.update(spb)
        for t in range(4):
            b = next_bank()
            MM([(bank_ap(b, 0, 256), rb16(hTo, kc * 512 + t * 128, n=128), AP(W1b, 0, 128, kc * 1792 + 256, [[1, 256]]), kc == 0, kc == 7, False)
                for kc in range(8)], W1B + [("hT", hb_i, t)], [("bank", b)])
            vi = t % 2
            ACT(rf32(o_sqv[vi], 0, n=256), bank_ap(b, 0, 256), AF.Square, [("bank", b)], [("sqv", vi)])
            tt = tile_ctr[0]
            tile_ctr[0] += 1
            g0 = (tt % 8) * 12
            P.op("dve", lambda e, vi=vi, g0=g0: e.tensor_reduce(out=gst[:, g0:g0 + 4], in_=rf32(o_sqv[vi], 0, dims=[[64, 4], [1, 64]]),
                                                                axis=AX.X, op=ALU.add),
                 [("sqv", vi)], [("gst", tt % 8, 0)])
            ACT(gst[:, g0 + 4:g0 + 8], gst[:, g0:g0 + 4], AF.Ln, [("gst", tt % 8, 0)], [("gst", tt % 8, 1)], scale=1.0 / 64, bias=EPS)
            ACT(gst[:, g0 + 8:g0 + 12], gst[:, g0 + 4:g0 + 8], AF.Exp, [("gst", tt % 8, 1)], [("gst", tt % 8, 2)], scale=-0.5)
            TT("dve", rb16(o_vn[vi], 0, dims=[[64, 4], [1, 64]]), bank_ap(b, 0, dims=[[64, 4], [1, 64]]),
               bass.AP(gst, g0 + 8, [[96, 128], [1, 4], [0, 64]]), ALU.mult, [("bank", b), ("gst", tt % 8, 2)], [("vn", vi)])
            MM([(bank_ap(spb[hh // 2], t * 128, 128, p0=64 * (hh % 2), npart=64), rb16(o_vn[vi], hh * 64, n=64),
                 wsTb[:, hh * 128:(hh + 1) * 128], True, True, True) for hh in range(4)],
               [("vn", vi), "wsT"], [("bank", spb[0]), ("bank", spb[1])])
        for c in range(2):
            gate_chunk(512 + c * 128, hTo, hkeys, rf32(o_sgg[sl2], c * 512, n=512), [("sgg", sl2, c)])
        for pp in range(2):
            gb = proj_f2(w1b_ap(pp * 128), W1B, hTo, 512, hkeys)
            fa = c2.fpool()
            STT(rf32(c2.f[fa], 0, dims=[[128, 4], [1, 128]]), bank_ap(spb[pp], 0, dims=[[128, 4], [1, 128]]), cs(C_VG + pp),
                rf32(o_bT, pp * 128, dims=[[0, 4], [1, 128]]), ALU.mult, ALU.add, [("bank", spb[pp]), "bT", "c_vg"], [("fp", fa)])
            fb = c2.fpool()
            TT("dve", rf32(c2.f[fb], 0, n=512), bank_ap(gb, 0, 512), rf32(c2.f[fa], 0, n=512), ALU.mult,
               [("bank", gb), ("fp", fa)], [("fp", fb)])
            TT("dve", rb16(o_yg[sl2], pp * 512, n=512), rf32(c2.f[fb], 0, n=512), rf32(o_sgg[sl2], pp * 512, n=512), ALU.mult,
               [("fp", fb), ("sgg", sl2, pp)], [("yg", sl2, pp)])
        held.clear()
        ykeys_g = [("yg", sl2, 0), ("yg", sl2, 1)]
        ykeys_am = [("QT", h, so) for h in range(8)] + [("ym", sl2, h) for h in range(4)]
        for t in range(4):
            ri = xr_ctr[0] % 3
            xr_ctr[0] += 1
            row0 = so * SL + t * 128
            xr_full = rf32(o_xr[ri], 0, n=1024)
            DMA("sp", xr_full, xo[row0:row0 + 128, :], [], [("xr", ri, 0), ("xr", ri, 1)])
            for half in range(2):
                b = next_bank()
                def lt_of(c):
                    if c < 2:
                        return rb16(o_yg[sl2], c * 512 + t * 128, n=128)
                    if c < 6:
                        return AP(QT, 0, 128, (c - 2) * NT + so * SL + t * 128, [[1, 128]])
                    return rb16(o_ym[sl2], (c - 6) * 512 + t * 128, n=128)
                MM([(bank_ap(b, 0, 512), lt_of(c), AP(Wout, 0, 128, c * 1024 + half * 512, [[1, 512]]), c == 2, False, False)
                    for c in (2, 3, 4, 5, 6, 7)], ykeys_am + WOUT, [("bank", b)])
                MM([(bank_ap(b, 0, 512), lt_of(c), AP(Wout, 0, 128, c * 1024 + half * 512, [[1, 512]]), False, c == 1, False)
                    for c in (0, 1)], ykeys_g + WOUT, [("bank", b)])
                xh_ = rf32(o_xr[ri], half * 512, n=512)
                TT("dve", xh_, bank_ap(b, 0, 512), xh_, ALU.add, [("bank", b), ("xr", ri, half)], [("xr", ri, half)])
            DMA("sp", out_d[row0:row0 + 128, :], xr_full, [("xr", ri, 0), ("xr", ri, 1)], [("out", row0)])

    xstate["pool"] = list(xpool) + [((lambda n, o_=o_: rf32(o_, 0, n=n)), ("xs", k_)) for k_, o_ in enumerate(o_xs)]
    nbanks[0] = 8
    P.default_prio = -40000.0
    preamble_1b()
    P.default_prio = 0.0
    for so in range(4):
        slab_1b(so)
    P.op("sp", lambda e: e.nop(), [("out", so * SL + t * 128) for so in range(4) for t in range(4)], [])
    P.emit()
    return nc


_NC_CACHE = {}


def kernel(x, mem, norm_gain, w_in, gmlp_v_gain, gmlp_w_s, gmlp_b, attn_q_gain, attn_k_gain,
           mem_norm_gain, w_mem_kv, mem_q_gain, mem_k_gain, w_out):
    f = np.float32
    x = np.asarray(x, f)
    mem = np.asarray(mem, f)
    B, S, D = x.shape
    w_in0 = np.ascontiguousarray(np.asarray(w_in, f)[0])
    w_out0 = np.ascontiguousarray(np.asarray(w_out, f)[0])
    w_kv0 = np.ascontiguousarray(np.asarray(w_mem_kv, f)[0])
    ng = np.ascontiguousarray(np.asarray(norm_gain, f)[0].reshape(8, 128).T)
    mng = np.ascontiguousarray(np.asarray(mem_norm_gain, f)[0].reshape(8, 128).T)
    vgn = np.asarray(gmlp_v_gain, f)[0]
    vg = np.ascontiguousarray(vgn.reshape(2, 128).T)
    ws = np.asarray(gmlp_w_s, f)[0]
    wsT = np.ascontiguousarray(ws.transpose(2, 0, 1).reshape(128, 512))
    bb = np.asarray(gmlp_b, f)[0]
    bT = np.ascontiguousarray(np.repeat(bb.reshape(2, 2, 1, 128), 64, axis=2).reshape(2, 128, 128).transpose(1, 0, 2).reshape(128, 256))
    ag = np.ascontiguousarray(np.stack([np.tile(np.asarray(attn_q_gain, f)[0], 2), np.tile(np.asarray(attn_k_gain, f)[0], 2)], 1))
    mg = np.ascontiguousarray(np.stack([np.tile(np.asarray(mem_q_gain, f)[0], 2), np.tile(np.asarray(mem_k_gain, f)[0], 2)], 1))
    ii = np.arange(128)
    ident = (ii[:, None] == ii[None, :]).astype(f)
    blkc = ((ii[:, None] // 64) == (ii[None, :] // 64)).astype(f) / 64.0
    cid = np.ascontiguousarray(np.concatenate([ident, blkc], axis=1))
    own = (ii[None, :] >= ii[:, None]).astype(f)
    prv = (ii[:, None] >= ii[None, :]).astype(f)
    cmask = np.ascontiguousarray(np.concatenate([own, prv, own, prv, own, prv, prv, own], axis=1))
    ctril = np.ascontiguousarray(np.tile(own, (1, 4)))
    if "nc" not in _NC_CACHE:
        _NC_CACHE["nc"] = build_nc()
    nc = _NC_CACHE["nc"]
    in_maps = []
    for c in range(8):
        b, half = c // 2, c % 2
        xo = np.ascontiguousarray(x[b, half * NT:(half + 1) * NT])
        if half == 0:
            xh = np.zeros((NT, D), f)
            hmv = np.zeros((128, 1), f)
        else:
            xh = np.ascontiguousarray(x[b, 0:NT])
            hmv = np.ones((128, 1), f)
        in_maps.append(dict(xo=xo, xh=xh, hm=hmv, mem=np.ascontiguousarray(mem[b]), w_in=w_in0, w_out=w_out0,
                            w_kv=w_kv0, ng=ng, mng=mng, vg=vg, wsT=wsT, bT=bT, ag=ag, mg=mg,
                            cid=cid, cmask=cmask, ctril=ctril))
    res = run_bass_kernel_spmd(nc, in_maps, core_ids=list(range(8)))
    out = np.empty((B, S, D), f)
    for c in range(8):
        b, half = c // 2, c % 2
        out[b, half * NT:(half + 1) * NT] = res.results[c]["out"]
    return out
```
